# Optimizing a Trainium2 kernel written in Bass

```python
import jax, jax.numpy as jnp
from jax import lax
import numpy as np

D_MODEL = 1024
BATCH = 32
SEQ = 2048
DEPTH = 4
DEC_BATCH = 8
DEC_SEQ = 32
PAST_LEN = 2048

CHUNK = 64
N_MIXERS = 2
EXPAND = 2
E_A = EXPAND * D_MODEL
E_B = EXPAND * D_MODEL
CONV_W = 3
POOL_WINDOWS = (2, 4, 8, 16)
N_POOL_GROUPS = len(POOL_WINDOWS)
G_B = E_B // N_POOL_GROUPS
POOL_HIST = max(POOL_WINDOWS) - 1
P_DIM = 256
N_A = (DEPTH + 1) // 2
N_B = DEPTH // 2
EPS = 1e-6

kernel_name = "hybrid_conv_pool_stream_step"


def rmsnorm(x, g):
    xf = x.astype(jnp.float32)
    r = lax.rsqrt(jnp.mean(xf * xf, axis=-1, keepdims=True) + EPS)
    return (xf * r).astype(x.dtype) * g


def conv_mixer(hn, hist, w_in, conv_w, w_out):
    proj = hn @ w_in
    b_gate, c_gate, h, z = jnp.split(proj, 4, axis=-1)
    v = c_gate * h
    L = v.shape[1]
    vp = jnp.concatenate([hist.astype(v.dtype), v], axis=1)
    y = conv_w[0] * vp[:, 0:L]
    for k in range(1, CONV_W):
        y = y + conv_w[k] * vp[:, k:k + L]
    y = b_gate * y * jax.nn.silu(z)
    return y @ w_out, vp[:, -(CONV_W - 1):]


def pool_mixer(hn, hist, pos0, w_in, w_grp, scale, w_out):
    proj = hn @ w_in
    u, z = jnp.split(proj, 2, axis=-1)
    Bsz, L, E = u.shape
    up = jnp.concatenate([hist.astype(u.dtype), u], axis=1)
    cs = jnp.cumsum(up.astype(jnp.float32), axis=1)
    cs = jnp.concatenate([jnp.zeros((Bsz, 1, E), jnp.float32), cs], axis=1)
    pos = pos0 + jnp.arange(L)
    uf = u.astype(jnp.float32)
    outs = []
    for j, w in enumerate(POOL_WINDOWS):
        sl = slice(j * G_B, (j + 1) * G_B)
        s = cs[:, POOL_HIST + 1:, sl] - cs[:, POOL_HIST + 1 - w:POOL_HIST + 1 - w + L, sl]
        cnt = jnp.minimum(pos + 1, w).astype(jnp.float32)
        outs.append(s / cnt[None, :, None] - uf[..., sl])
    d = jnp.stack(outs, axis=2).astype(u.dtype)
    mixed = jnp.einsum('blgc,gcd->blgd', d, w_grp).reshape(Bsz, L, E) * scale
    y = mixed * jax.nn.silu(z)
    return y @ w_out, up[:, -POOL_HIST:]


def trunk(x, p, conv_hist, pool_hist, pos0, norm_g, w_in_a, conv_w_a, w_out_a,
          w_in_b, w_grp_b, scale_b, w_out_b, w_pe, w_pg, final_g):
    h = x
    conv_new = []
    pool_new = []
    for i in range(DEPTH):
        hn = rmsnorm(h, norm_g[i])
        k = i // N_MIXERS
        if i % N_MIXERS == 0:
            out, st = conv_mixer(hn, conv_hist[k], w_in_a[k], conv_w_a[k], w_out_a[k])
            conv_new.append(st)
        else:
            out, st = pool_mixer(hn, pool_hist[k], pos0, w_in_b[k], w_grp_b[k],
                                 scale_b[k], w_out_b[k])
            pool_new.append(st)
        h = h + out
        h = h + (p[i] @ w_pe[i]) * jax.nn.sigmoid(h @ w_pg[i])
    return rmsnorm(h, final_g), jnp.stack(conv_new), jnp.stack(pool_new)


def setup_inputs(seed: int = 0) -> dict:
    key = jax.random.key(seed)
    ks = jax.random.split(key, 20)
    f32 = jnp.float32
    nrm = lambda k, s, sc: jax.random.normal(k, s, f32) * sc
    return {
        "x_prompt": nrm(ks[0], (BATCH, SEQ, D_MODEL), 1.0),
        "x_sample": nrm(ks[1], (DEC_BATCH, DEC_SEQ, D_MODEL), 1.0),
        "state_conv": nrm(ks[2], (N_A, DEC_BATCH, CONV_W - 1, E_A), 1.0),
        "state_pool": nrm(ks[3], (N_B, DEC_BATCH, POOL_HIST, E_B), 1.0),
        "p_prompt": nrm(ks[4], (DEPTH, BATCH, SEQ, P_DIM), 1.0),
        "p_sample": nrm(ks[5], (DEPTH, DEC_BATCH, DEC_SEQ, P_DIM), 1.0),
        "norm_g": 1.0 + nrm(ks[6], (DEPTH, D_MODEL), 0.05),
        "w_in_a": nrm(ks[7], (N_A, D_MODEL, 4 * E_A), D_MODEL ** -0.5),
        "conv_w_a": nrm(ks[8], (N_A, CONV_W, E_A), CONV_W ** -0.5),
        "w_out_a": nrm(ks[9], (N_A, E_A, D_MODEL), 0.5 * E_A ** -0.5),
        "w_in_b": nrm(ks[10], (N_B, D_MODEL, 2 * E_B), D_MODEL ** -0.5),
        "w_grp_b": nrm(ks[11], (N_B, N_POOL_GROUPS, G_B, G_B), G_B ** -0.5),
        "scale_b": 1.0 + nrm(ks[12], (N_B, E_B), 0.1),
        "w_out_b": nrm(ks[13], (N_B, E_B, D_MODEL), 0.5 * E_B ** -0.5),
        "w_pe": nrm(ks[14], (DEPTH, P_DIM, D_MODEL), P_DIM ** -0.5),
        "w_pg": nrm(ks[15], (DEPTH, D_MODEL, D_MODEL), D_MODEL ** -0.5),
        "final_g": 1.0 + nrm(ks[16], (D_MODEL,), 0.05),
    }


def reference(x_prompt, x_sample, state_conv, state_pool, p_prompt, p_sample, norm_g,
              w_in_a, conv_w_a, w_out_a, w_in_b, w_grp_b, scale_b, w_out_b,
              w_pe, w_pg, final_g):
    bp = x_prompt.shape[0]
    conv0 = jnp.zeros((N_A, bp, CONV_W - 1, E_A), x_prompt.dtype)
    pool0 = jnp.zeros((N_B, bp, POOL_HIST, E_B), x_prompt.dtype)
    y_prompt, conv_state_prompt, pool_state_prompt = trunk(
        x_prompt, p_prompt, conv0, pool0, 0, norm_g, w_in_a, conv_w_a, w_out_a,
        w_in_b, w_grp_b, scale_b, w_out_b, w_pe, w_pg, final_g)
    y_sample, conv_state_sample, pool_state_sample = trunk(
        x_sample, p_sample, state_conv, state_pool, PAST_LEN, norm_g, w_in_a, conv_w_a,
        w_out_a, w_in_b, w_grp_b, scale_b, w_out_b, w_pe, w_pg, final_g)
    return (y_prompt, y_sample, conv_state_prompt, pool_state_prompt,
            conv_state_sample, pool_state_sample)
```

```python
import numpy as np
import concourse.bass as bass
import concourse.mybir as mybir
from concourse.bass_utils import run_bass_kernel_spmd

F32 = mybir.dt.float32
BF16 = mybir.dt.bfloat16
AF = mybir.ActivationFunctionType
ALU = mybir.AluOpType

NCORE = 8
D = 1024
NC_D = 8
E = 2048
NC_E = 16
DEPTH = 4
SEQ = 2048
TP = 1024
TS = 32
NTILE_P = 8
EPS = 1e-6
WINDOWS = (2, 4, 8, 16)
NSLOT = 5
WBLK = 4096
ROWW = 1040
HALF = 520
NROW = 8
STAGGER = True
WARM_A = 10
WARM_B = 18

C_G = 0
C_FG = 32
C_CW = 40
C_SC = 136
C_RC = 168
NCONST = 184


def block_schedule():
    bl = []
    for l in range(DEPTH):
        if l % 2 == 0:
            for j in range(NC_E):
                bl.append(("ain", l, j, 4096))
        else:
            for kind, g in (("bu", 0), ("bu", 1), ("bz", 0), ("bg", 0), ("bu", 2), ("bz", 1),
                            ("bg", 1), ("bu", 3), ("bz", 2), ("bg", 2), ("bz", 3), ("bg", 3)):
                bl.append((kind, l, g, 2048 if kind == "bg" else 4096))
        for mb in range(4):
            bl.append(("out", l, mb, 4096))
        for mb in range(2):
            bl.append(("pg", l, mb, 4096))
        bl.append(("pe", l, 0, 2048))
    return bl


SCHED = block_schedule()
WOFF = np.concatenate([[0], np.cumsum([b[3] for b in SCHED])]).astype(np.int64)
WTOT = int(WOFF[-1])


class Sem:
    __slots__ = ("h", "n")

    def __init__(self, h):
        self.h = h
        self.n = 0


class Buf:
    __slots__ = ("w", "r")

    def __init__(self):
        self.w = None
        self.r = []


class Gen:
    ENG = ("pe", "act", "dve", "pool", "sp")

    def __init__(self, sem_handles):
        self.free_sems = list(sem_handles)
        self.ops = {e: [] for e in self.ENG}
        self.seen = {e: {} for e in self.ENG}
        self.prog = {e: self.new_sem() for e in ("pe", "act", "dve", "pool")}
        self.dma_sems = []

    def new_sem(self):
        return Sem(self.free_sems.pop(0))

    def new_dma_sem(self):
        s = self.new_sem()
        self.dma_sems.append(s)
        return s

    def _wait(self, e, ev):
        s, v = ev
        own = self.prog.get(e)
        if s is own:
            if e == "pe":
                return
            if v < s.n - 1:
                return
        if self.seen[e].get(s, 0) >= v:
            return
        self.seen[e][s] = v
        self.ops[e].append(("wait", s, v))

    def _deps(self, e, reads, writes):
        for b in reads:
            if b.w is not None:
                self._wait(e, b.w)
        for b in writes:
            if b.w is not None:
                self._wait(e, b.w)
            for ev in b.r:
                self._wait(e, ev)

    @staticmethod
    def _mark(ev, reads, writes):
        s = ev[0]
        for b in reads:
            b.r = [x for x in b.r if x[0] is not s] + [ev]
        for b in writes:
            b.w = ev
            b.r = []

    def emit(self, e, fn, reads=(), writes=(), inc=True):
        self._deps(e, reads, writes)
        s = self.prog[e]
        if inc:
            s.n += 1
            ev = (s, s.n)
            self.ops[e].append(("ins", fn, s, 1))
        else:
            ev = (s, s.n + 1)
            self.ops[e].append(("ins", fn, None, 0))
        self._mark(ev, reads, writes)

    def dma(self, q, out_ap, in_ap, sem, reads=(), writes=()):
        self._deps(q, reads, writes)
        if sem.n > 0:
            self._wait(q, (sem, sem.n))
        sem.n += 16
        ev = (sem, sem.n)
        self.ops[q].append(("ins", (lambda eng: eng.dma_start(out=out_ap, in_=in_ap)), sem, 16))
        self._mark(ev, reads, writes)

    def finish(self, q="sp"):
        for s in self.dma_sems:
            if s.n > 0:
                self._wait(q, (s, s.n))

    def replay(self, e, eng):
        for op in self.ops[e]:
            if op[0] == "wait":
                eng.wait_ge(op[1].h, op[2])
            else:
                ins = op[1](eng)
                if op[2] is not None:
                    ins.then_inc(op[2].h, op[3])

    def mm(self, out, lhsT, rhs, start, stop, reads, writes, flag):
        self.emit("pe", (lambda t: t.matmul(out, lhsT, rhs, start=start, stop=stop)),
                  reads, writes, inc=flag)

    def act(self, out, in_, func, reads, writes, scale=None, bias=None):
        kw = {}
        if scale is not None:
            kw["scale"] = scale
        if bias is not None:
            kw["bias"] = bias
        self.emit("act", (lambda a: a.activation(out=out, in_=in_, func=func, **kw)), reads, writes)

    def tt(self, out, in0, in1, op, reads, writes):
        self.emit("dve", (lambda v: v.tensor_tensor(out=out, in0=in0, in1=in1, op=op)), reads, writes)

    def stt(self, out, in0, scalar, in1, op0, op1, reads, writes):
        self.emit("dve", (lambda v: v.scalar_tensor_tensor(out=out, in0=in0, scalar=scalar, in1=in1,
                                                           op0=op0, op1=op1)), reads, writes)

    def recip(self, out, in_, reads, writes):
        self.emit("dve", (lambda v: v.reciprocal(out=out, in_=in_)), reads, writes)

    def memset(self, e, ap, val, writes):
        self.emit(e, (lambda v: v.memset(ap, val)), (), writes)


class Tile:
    def __init__(self, T, x_src, p_src, y_dst, first, last, is_prompt, st_out):
        self.T = T
        self.x_src = x_src
        self.p_src = p_src
        self.y_dst = y_dst
        self.first = first
        self.last = last
        self.is_prompt = is_prompt
        self.st_out = st_out
        self.segs = [(s, min(512, T - s)) for s in range(0, T, 512)]


def build_program():
    nc = bass.Bass("TRN2", target_bir_lowering=False)
    dt = nc.dram_tensor
    xp = dt("xp", [NTILE_P, 128, NC_D, 2, 512], F32, kind="ExternalInput").ap()
    xs = dt("xs", [128, NC_D, TS], F32, kind="ExternalInput").ap()
    pp = dt("pp", [NTILE_P, DEPTH, 128, 2, TP], F32, kind="ExternalInput").ap()
    psm = dt("psm", [DEPTH, 128, 2, TS], F32, kind="ExternalInput").ap()
    sci = dt("sci", [128, 2, NC_E, 2], F32, kind="ExternalInput").ap()
    spi = dt("spi", [128, 2, NC_E, 15], F32, kind="ExternalInput").ap()
    cstd = dt("cst", [128, NCONST], F32, kind="ExternalInput").ap()
    wst = dt("wst", [128, WTOT], F32, kind="ExternalInput").ap()
    yp = dt("yp", [NTILE_P, 128, NC_D, 2, 512], F32, kind="ExternalOutput").ap()
    ys = dt("ys", [128, NC_D, TS], F32, kind="ExternalOutput").ap()
    cop = dt("cop", [4, 128, 2, NC_E, 2], F32, kind="ExternalOutput").ap()
    pop = dt("pop", [4, 128, 2, NC_E, 15], F32, kind="ExternalOutput").ap()
    cos = dt("cos", [128, 2, NC_E, 2], F32, kind="ExternalOutput").ap()
    pos = dt("pos", [128, 2, NC_E, 15], F32, kind="ExternalOutput").ap()
    wcache = dt("wcache", [128, WTOT], BF16, kind="Internal").ap()

    import contextlib
    with contextlib.ExitStack() as es:
        sb = lambda name, shape, dtp: es.enter_context(nc.sbuf_tensor(name, shape, dtp))
        BUF = [sb("bufA", [128, NC_E, TP], BF16), sb("bufB", [128, NC_E, TP], BF16)]
        Hf = [b_.bitcast(F32) for b_ in BUF]
        xb = sb("xb", [128, NC_D, TP], BF16)
        pb = sb("pb", [128, 2, 2, TP], BF16)
        wr = sb("wr", [128, NSLOT, WBLK], BF16)
        sq = sb("sq", [128, 4, 2, 512], BF16)
        rows = sb("rows", [128, NROW, ROWW], F32)
        dbuf = sb("dbuf", [128, 2, 4, TP], BF16)
        cst = sb("cstt", [128, NCONST], F32)
        cconv = sb("cconv", [128, 2, 2, NC_E, 2], F32)
        cpool = sb("cpool", [128, 2, 2, NC_E, 15], F32)
        ones = sb("ones", [128, 128], BF16)
        epsb = sb("epsb", [128, 1], F32)
        fix = sb("fix", [128, 16], F32)
        junk = sb("junk", [128, 512], BF16)
        SW = 64
        BUF_s = [sb("bufAs", [128, NC_E, SW], BF16), sb("bufBs", [128, NC_E, SW], BF16)]
        Hf_s = [b_.bitcast(F32) for b_ in BUF_s]
        xb_s = sb("xbs", [128, NC_D, SW], BF16)
        pb_s = sb("pbs", [128, 2, 2, SW], BF16)
        sq_s = sb("sqs", [128, 4, 2, SW], BF16)
        rows_s = sb("rowss", [128, NROW, 2 * SW], F32)
        dbuf_s = sb("dbufs", [128, 2, 4, SW], BF16)
        fix_s = sb("fixs", [128, 16], F32)
        ps = es.enter_context(nc.psum_tensor("ps", [128, 8, 512], F32))
        sem_handles = [es.enter_context(nc.semaphore("s%d" % i)) for i in range(40)]
        block = es.enter_context(nc.Block())

        g = Gen(sem_handles)
        P = [[Buf() for _ in range(NC_D)] for _ in range(2)]
        xb_b = [[Buf(), Buf()] for _ in range(NC_D)]
        pb_b = [Buf(), Buf()]
        wr_b = [Buf() for _ in range(NSLOT)]
        sq_b = [Buf() for _ in range(4)]
        row_b = [[Buf(), Buf()] for _ in range(NROW)]
        db_b = [Buf(), Buf()]
        cst_b = Buf()
        cconv_b = [[Buf(), Buf()], [Buf(), Buf()]]
        cpool_b = [[Buf(), Buf()], [Buf(), Buf()]]
        ones_b = Buf()
        eps_b = Buf()
        fix_b = Buf()
        junk_b = Buf()
        bank_b = [Buf() for _ in range(8)]
        s_w = [g.new_dma_sem() for _ in range(NSLOT)]
        s_x = [g.new_dma_sem() for _ in range(NC_D)]
        s_p = [g.new_dma_sem(), g.new_dma_sem()]
        s_yo = [g.new_dma_sem() for _ in range(NC_D)]
        s_c = g.new_dma_sem()
        s_sti = [g.new_dma_sem(), g.new_dma_sem()]
        s_sto = [g.new_dma_sem(), g.new_dma_sem()]
        s_wb = [g.new_dma_sem() for _ in range(NSLOT)]
        cache_b = [Buf() for _ in range(NSLOT)]

        class Cx:
            pass

        cxp = Cx()
        cxp.BUF, cxp.Hf, cxp.P = BUF, Hf, P
        cxp.xb, cxp.xb_b, cxp.pb, cxp.pb_b, cxp.s_p = xb, xb_b, pb, pb_b, s_p
        cxp.sq, cxp.sq_b, cxp.rows, cxp.row_b = sq, sq_b, rows, row_b
        cxp.dbuf, cxp.db_b, cxp.fix, cxp.fix_b = dbuf, db_b, fix, fix_b
        cxp.HALF, cxp.unit = HALF, 0
        cxs = Cx()
        cxs.BUF, cxs.Hf = BUF_s, Hf_s
        cxs.P = [[Buf() for _ in range(NC_D)] for _ in range(2)]
        cxs.xb, cxs.xb_b = xb_s, [[Buf(), Buf()] for _ in range(NC_D)]
        cxs.pb, cxs.pb_b, cxs.s_p = pb_s, [Buf(), Buf()], [g.new_dma_sem(), g.new_dma_sem()]
        cxs.sq, cxs.sq_b = sq_s, [Buf() for _ in range(4)]
        cxs.rows, cxs.row_b = rows_s, [[Buf(), Buf()] for _ in range(NROW)]
        cxs.dbuf, cxs.db_b, cxs.fix, cxs.fix_b = dbuf_s, [Buf(), Buf()], fix_s, Buf()
        cxs.HALF, cxs.unit = SW, 0

        state = {"bank": 0, "blk": 0, "hold": set(), "side": [], "pending": [],
                 "log": [], "logging": False, "replay": False}

        def mark(label):
            MARKS.append((label, g.prog["pe"].n, g.prog["act"].n, g.prog["dve"].n,
                          sum(1 for o in g.ops["pe"] if o[0] == "ins")))

        def next_bank():
            b = state["bank"]
            while b in state["hold"]:
                b = (b + 1) % 8
            state["bank"] = (b + 1) % 8
            return b

        def next_block(kind):
            if state["replay"]:
                k_, s_ = state["log"].pop(0)
                assert k_ == kind, (k_, kind)
                return s_
            i = state["blk"]
            state["blk"] = i + 1
            bi = i % len(SCHED)
            assert SCHED[bi][0] == kind, (SCHED[bi], kind)
            s = i % NSLOT
            n = SCHED[bi][3]
            off = int(WOFF[bi])
            if i < len(SCHED):
                first_rd = tuple(P[0][c] for c in range(NC_D)) if i == 0 else ()
                g.dma("pool", wr[:, s, 0:n], wst[:, off:off + n], s_w[s], reads=first_rd, writes=(wr_b[s],))
                g.dma("sp", wcache[:, off:off + n], wr[:, s, 0:n], s_wb[s], reads=(wr_b[s],), writes=(cache_b[s],))
            else:
                g.dma("pool", wr[:, s, 0:n], wcache[:, off:off + n], s_w[s], reads=(cache_b[bi % NSLOT],),
                      writes=(wr_b[s],))
            if state["logging"]:
                state["log"].append((kind, s))
            return s

        def cc(idx):
            return cst[:, idx:idx + 1]

        def hs(tile, c, si, n):
            return tile.cx.Hf[tile.hb][:, 2 * c + si, 0:n]

        def hfull(tile, c):
            if tile.T == TP:
                return tile.cx.Hf[tile.hb][:, 2 * c:2 * c + 2, :]
            return tile.cx.Hf[tile.hb][:, 2 * c, 0:tile.T]

        def yv(tile, j, cs, n):
            return tile.cx.BUF[1 - tile.hb][:, j, cs:cs + n]

        def hB(tile, c):
            return tile.cx.P[tile.hb][c]

        def yB(tile, j):
            return tile.cx.P[1 - tile.hb][j // 2]

        def rr(tile, cs, n):
            return tile.cx.rows[:, 6 + tile.hb, cs:cs + n]

        def rrB(tile):
            return (tile.cx.row_b[6 + tile.hb][0], tile.cx.row_b[6 + tile.hb][1])

        g.dma("sp", cst[:, :], cstd, s_c, writes=(cst_b,))
        g.memset("dve", ones[:, :], 1.0, (ones_b,))
        g.memset("dve", epsb[:, :], EPS, (eps_b,))
        g.memset("dve", junk[:, :], 1.0, (junk_b,))
        g.memset("dve", rows[:, :, :], 0.0, [b for r in row_b for b in r])
        g.memset("dve", rows_s[:, :, :], 0.0, [b for r in cxs.row_b for b in r])

        def load_p(tile, l):
            pb, pb_b, s_p = tile.cx.pb, tile.cx.pb_b, tile.cx.s_p
            par = l % 2
            g.dma("pool", pb[:, par, :, 0:tile.T], tile.p_src[l], s_p[par], writes=(pb_b[par],))

        def load_x(tile, c):
            g.dma("sp", hfull(tile, c), tile.x_src[:, c], s_x[c], writes=(hB(tile, c),))

        def init_carries(tile):
            cp = tile.cpar
            if tile.is_prompt:
                for k in range(2):
                    g.memset("dve", cconv[:, cp, k, :, :], 0.0, (cconv_b[cp][k],))
                    g.memset("dve", cpool[:, cp, k, :, :], 0.0, (cpool_b[cp][k],))
            else:
                g.dma("sp", cconv[:, cp, :, :, :], sci, s_sti[0], writes=(cconv_b[cp][0], cconv_b[cp][1]))
                g.dma("sp", cpool[:, cp, :, :, :], spi, s_sti[1], writes=(cpool_b[cp][0], cpool_b[cp][1]))

        def stats_begin(tile, sset):
            bks = [next_bank() for _ in tile.segs]
            state["hold"] |= set(bks)
            return {"tile": tile, "banks": bks, "set": sset}

        def sq_idx(ss, c):
            return ss["set"] * 2 + (c % 2)

        def stats_act(ss, c):
            tile = ss["tile"]
            sq, sq_b = tile.cx.sq, tile.cx.sq_b
            qi = sq_idx(ss, c)
            out = sq[:, qi, :, :] if tile.T == TP else sq[:, qi, 0, 0:tile.T]
            g.act(out, hfull(tile, c), AF.Square, (hB(tile, c),), (sq_b[qi],))

        def stats_pe(ss, c):
            tile = ss["tile"]
            sq, sq_b = tile.cx.sq, tile.cx.sq_b
            qi = sq_idx(ss, c)
            for si, (cs, n) in enumerate(tile.segs):
                bk = ss["banks"][si]
                g.mm(ps[:, bk, 0:n], ones[:, :], sq[:, qi, si, 0:n], c == 0, c == NC_D - 1,
                     (ones_b, sq_b[qi]), (bank_b[bk],), flag=True)

        def stats_act_seg(ss, c, si):
            tile = ss["tile"]
            sq, sq_b = tile.cx.sq, tile.cx.sq_b
            qi = sq_idx(ss, c)
            cs, n = tile.segs[si]
            g.act(sq[:, qi, si, 0:n], hs(tile, c, si, n), AF.Square, (hB(tile, c),), (sq_b[qi],))

        def stats_pe_seg(ss, c, si):
            tile = ss["tile"]
            sq, sq_b = tile.cx.sq, tile.cx.sq_b
            qi = sq_idx(ss, c)
            cs, n = tile.segs[si]
            bk = ss["banks"][si]
            g.mm(ps[:, bk, 0:n], ones[:, :], sq[:, qi, si, 0:n], c == 0, c == NC_D - 1,
                 (ones_b, sq_b[qi]), (bank_b[bk],), flag=True)

        def stats_end_seg(ss, si):
            tile = ss["tile"]
            rows, row_b = tile.cx.rows, tile.cx.row_b
            r4 = (row_b[4][0], row_b[4][1])
            cs, n = tile.segs[si]
            bk = ss["banks"][si]
            g.act(rows[:, 4, cs:cs + n], ps[:, bk, 0:n], AF.Ln, (bank_b[bk], eps_b), r4,
                  scale=1.0 / D, bias=epsb[:, 0:1])
            g.act(rr(tile, cs, n), rows[:, 4, cs:cs + n], AF.Exp, r4, rrB(tile), scale=-0.5)
            state["hold"] -= {bk}

        def run_pending():
            jobs = state["pending"]
            state["pending"] = []
            for jb in jobs:
                jb()

        def stag(tile):
            return STAGGER and len(tile.segs) == 2

        def stats_end(ss):
            tile = ss["tile"]
            rows, row_b = tile.cx.rows, tile.cx.row_b
            r4 = (row_b[4][0], row_b[4][1])
            for si, (cs, n) in enumerate(tile.segs):
                bk = ss["banks"][si]
                g.act(rows[:, 4, cs:cs + n], ps[:, bk, 0:n], AF.Ln, (bank_b[bk], eps_b),
                      r4, scale=1.0 / D, bias=epsb[:, 0:1])
            for si, (cs, n) in enumerate(tile.segs):
                g.act(rr(tile, cs, n), rows[:, 4, cs:cs + n], AF.Exp, r4, rrB(tile), scale=-0.5)
            state["hold"] -= set(ss["banks"])

        def norm_apply_seg(tile, l, si):
            xb, xb_b = tile.cx.xb, tile.cx.xb_b
            if si >= len(tile.segs):
                return
            cs, n = tile.segs[si]
            for c in range(NC_D):
                g.stt(xb[:, c, cs:cs + n], hs(tile, c, si, n), cc(C_G + l * 8 + c), rr(tile, cs, n),
                      ALU.mult, ALU.mult, (hB(tile, c), cst_b) + rrB(tile), (xb_b[c][si],))

        def norm_apply(tile, l):
            for si in range(len(tile.segs)):
                norm_apply_seg(tile, l, si)

        def norm_apply_jobs(tile, l, si):
            xb, xb_b = tile.cx.xb, tile.cx.xb_b
            cs, n = tile.segs[si]

            def mk(c):
                return lambda: g.stt(xb[:, c, cs:cs + n], hs(tile, c, si, n), cc(C_G + l * 8 + c), rr(tile, cs, n),
                                     ALU.mult, ALU.mult, (hB(tile, c), cst_b) + rrB(tile), (xb_b[c][si],))
            return [mk(c) for c in range(NC_D)]

        def keep_warm(nmm):
            if nmm <= 0:
                return
            bk = next_bank()
            for i in range(nmm):
                g.mm(ps[:, bk, 0:512], ones[:, :], junk[:, :], True, True, (ones_b, junk_b), (bank_b[bk],),
                     flag=(i == nmm - 1))

        def pop_side():
            if state["side"]:
                state["side"].pop(0)()

        def flush_side():
            while state["side"]:
                state["side"].pop(0)()

        def mixer_a(tile, l, staggered):
            xb, xb_b = tile.cx.xb, tile.cx.xb_b
            rows, row_b = tile.cx.rows, tile.cx.row_b
            HALF = tile.cx.HALF
            k = l // 2
            T = tile.T
            cp = tile.cpar

            def hist(j):
                vr = j % 2
                g.act(rows[:, vr, 0:2], cconv[:, cp, k, j, :], AF.Copy, (cconv_b[cp][k],),
                      (row_b[vr][0], row_b[vr][1]))

            def carry(j):
                vr = j % 2
                g.act(cconv[:, cp, k, j, :], rows[:, vr, T:T + 2], AF.Copy, (row_b[vr][0], row_b[vr][1]),
                      (cconv_b[cp][k],))

            def unit_a(j, si, s, kc_outer):
                cs, n = tile.segs[si]
                vr = j % 2
                vbufs = (row_b[vr][0], row_b[vr][1])
                u = tile.cx.unit
                tile.cx.unit = u + 1
                hp = u % 2
                ho = hp * HALF
                bks = [next_bank() for _ in range(4)]
                if kc_outer:
                    order = [(wi, kc) for kc in range(NC_D) for wi in range(4)]
                else:
                    order = [(wi, kc) for wi in range(4) for kc in range(NC_D)]
                for wi, kc in order:
                    g.mm(ps[:, bks[wi], 0:n], wr[:, s, (kc * 4 + wi) * 128:(kc * 4 + wi + 1) * 128],
                         xb[:, kc, cs:cs + n], kc == 0, kc == NC_D - 1,
                         (wr_b[s], xb_b[kc][si]), (bank_b[bks[wi]],), flag=(kc == NC_D - 1))
                bb, bc, bh, bz = bks
                hh = rows[:, 2, ho:ho + n]
                szv = rows[:, 3, ho:ho + n]
                acc = rows[:, 4, ho:ho + n]
                gz = rows[:, 5, ho:ho + n]
                g.act(hh, ps[:, bh, 0:n], AF.Copy, (bank_b[bh],), (row_b[2][hp],))
                g.tt(rows[:, vr, 2 + cs:2 + cs + n], ps[:, bc, 0:n], hh, ALU.mult,
                     (bank_b[bc], row_b[2][hp]), vbufs)
                g.act(szv, ps[:, bz, 0:n], AF.Silu, (bank_b[bz],), (row_b[3][hp],))
                g.tt(gz, ps[:, bb, 0:n], szv, ALU.mult, (bank_b[bb], row_b[3][hp]), (row_b[5][hp],))
                cw = C_CW + (k * 3) * 16 + j
                g.act(acc, rows[:, vr, cs:cs + n], AF.Copy, vbufs + (cst_b,), (row_b[4][hp],), scale=cc(cw))
                g.stt(acc, rows[:, vr, cs + 1:cs + 1 + n], cc(cw + 16), acc, ALU.mult, ALU.add,
                      vbufs + (cst_b, row_b[4][hp]), (row_b[4][hp],))
                g.stt(acc, rows[:, vr, cs + 2:cs + 2 + n], cc(cw + 32), acc, ALU.mult, ALU.add,
                      vbufs + (cst_b, row_b[4][hp]), (row_b[4][hp],))
                g.tt(yv(tile, j, cs, n), acc, gz, ALU.mult, (row_b[4][hp], row_b[5][hp]), (yB(tile, j),))
                pop_side()

            j0 = 0
            if staggered or len(tile.segs) == 2:
                s0 = next_block("ain")
                hist(0)
                s1 = next_block("ain")
                hist(1)
                unit_a(0, 0, s0, not staggered)
                run_pending()
                unit_a(1, 0, s1, False)
                unit_a(0, 1, s0, True)
                unit_a(1, 1, s1, False)
                carry(0)
                carry(1)
                j0 = 2
                yield
            for j in range(j0, NC_E):
                s = next_block("ain")
                hist(j)
                for si in range(len(tile.segs)):
                    unit_a(j, si, s, (not staggered) and j == 0 and si == 0)
                carry(j)
                yield

        def mixer_b(tile, l, staggered):
            xb, xb_b = tile.cx.xb, tile.cx.xb_b
            rows, row_b = tile.cx.rows, tile.cx.row_b
            dbuf, db_b, fix, fix_b = tile.cx.dbuf, tile.cx.db_b, tile.cx.fix, tile.cx.fix_b
            k = l // 2
            T = tile.T
            EE = 16 + T
            cp = tile.cpar

            def u_prep(gi):
                w = WINDOWS[gi]
                s = next_block("bu")
                dp = gi % 2
                sa = (row_b[2][0], row_b[2][1])
                sbb = (row_b[3][0], row_b[3][1])

                def u_pe(jjs, kc_outer, pairs=None):
                    if pairs is None:
                        pairs = [(jj, si) for jj in jjs for si in range(len(tile.segs))]
                    units = [(jj, tile.segs[si][0], tile.segs[si][1], next_bank(), si) for (jj, si) in pairs]
                    if kc_outer:
                        order = [(ui, kc) for si_ in range(len(tile.segs)) for kc in range(NC_D)
                                 for ui in range(len(units)) if units[ui][4] == si_]
                    else:
                        order = [(ui, kc) for ui in range(len(units)) for kc in range(NC_D)]
                    for ui, kc in order:
                        jj, cs, n, bk, si = units[ui]
                        g.mm(ps[:, bk, 0:n], wr[:, s, (kc * 4 + jj) * 128:(kc * 4 + jj + 1) * 128],
                             xb[:, kc, cs:cs + n], kc == 0, kc == NC_D - 1,
                             (wr_b[s], xb_b[kc][si]), (bank_b[bk],), flag=(kc == NC_D - 1))
                    return units

                def u_cons(jj, units):
                    j = gi * 4 + jj
                    ur = j % 2
                    ub = (row_b[ur][0], row_b[ur][1])
                    g.act(rows[:, ur, 1:16], cpool[:, cp, k, j, :], AF.Copy, (cpool_b[cp][k],), ub)
                    for (ujj, cs, n, bk, _si) in units:
                        if ujj != jj:
                            continue
                        g.act(rows[:, ur, 16 + cs:16 + cs + n], ps[:, bk, 0:n], AF.Copy, (bank_b[bk],), ub)
                        g.tt(rows[:, 2, 16 + cs:16 + cs + n], ps[:, bk, 0:n], rows[:, ur, 15 + cs:15 + cs + n],
                             ALU.add, (bank_b[bk],) + ub, sa)
                    if w >= 4:
                        g.tt(rows[:, 2, 2:16], rows[:, ur, 2:16], rows[:, ur, 1:15], ALU.add, ub, sa)
                        g.tt(rows[:, 3, 4:EE], rows[:, 2, 4:EE], rows[:, 2, 2:EE - 2], ALU.add, sa, sbb)
                        S, Sb = 3, sbb
                    else:
                        S, Sb = 2, sa
                    if w >= 8:
                        g.tt(rows[:, 2, 8:EE], rows[:, 3, 8:EE], rows[:, 3, 4:EE - 4], ALU.add, sbb, sa)
                        S, Sb = 2, sa
                    if w >= 16:
                        g.tt(rows[:, 3, 16:EE], rows[:, 2, 16:EE], rows[:, 2, 8:EE - 8], ALU.add, sa, sbb)
                        S, Sb = 3, sbb
                    g.stt(dbuf[:, dp, jj, 0:T], rows[:, S, 16:EE], 1.0 / w, rows[:, ur, 16:EE],
                          ALU.mult, ALU.subtract, Sb + ub, (db_b[dp],))
                    if tile.is_prompt and tile.first:
                        nf = w - 1
                        g.tt(fix[:, 0:nf], rows[:, S, 16:16 + nf], cst[:, C_RC:C_RC + nf], ALU.mult,
                             Sb + (cst_b,), (fix_b,))
                        g.tt(dbuf[:, dp, jj, 0:nf], fix[:, 0:nf], rows[:, ur, 16:16 + nf], ALU.subtract,
                             (fix_b,) + ub, (db_b[dp],))
                    g.act(cpool[:, cp, k, j, :], rows[:, ur, T + 1:T + 16], AF.Copy, ub, (cpool_b[cp][k],))

                return u_pe, u_cons

            def zm_prep(gi, zrows=None):
                sz_ = next_block("bz")
                sg = next_block("bg")
                dp = gi % 2

                def zrow(j):
                    if zrows is not None:
                        return zrows[j % 4]
                    return 4 + (j % 2)

                def zpart(jj):
                    j = gi * 4 + jj
                    zr = zrow(j)
                    zb = (row_b[zr][0], row_b[zr][1])
                    for si, (cs, n) in enumerate(tile.segs):
                        bz = next_bank()
                        for kc in range(NC_D):
                            g.mm(ps[:, bz, 0:n], wr[:, sz_, (kc * 4 + jj) * 128:(kc * 4 + jj + 1) * 128],
                                 xb[:, kc, cs:cs + n], kc == 0, kc == NC_D - 1,
                                 (wr_b[sz_], xb_b[kc][si]), (bank_b[bz],), flag=(kc == NC_D - 1))
                        g.act(rows[:, zr, cs:cs + n], ps[:, bz, 0:n], AF.Silu, (bank_b[bz],), zb)

                def mpart(jj):
                    j = gi * 4 + jj
                    zr = zrow(j)
                    zb = (row_b[zr][0], row_b[zr][1])
                    for si, (cs, n) in enumerate(tile.segs):
                        bm = next_bank()
                        for kc in range(4):
                            g.mm(ps[:, bm, 0:n], wr[:, sg, (kc * 4 + jj) * 128:(kc * 4 + jj + 1) * 128],
                                 dbuf[:, dp, kc, cs:cs + n], kc == 0, kc == 3,
                                 (wr_b[sg], db_b[dp]), (bank_b[bm],), flag=(kc == 3))
                        g.stt(yv(tile, j, cs, n), ps[:, bm, 0:n], cc(C_SC + k * 16 + j), rows[:, zr, cs:cs + n],
                              ALU.mult, ALU.mult, (bank_b[bm], cst_b) + zb, (yB(tile, j),))

                def step(jj):
                    zpart(jj)
                    mpart(jj)
                step.zpart = zpart
                step.mpart = mpart
                return step

            mark("L%d u0" % l)
            u_pe, u_cons = u_prep(0)
            if staggered:
                ua = u_pe(None, False, pairs=[(0, 0)])
                run_pending()
                ua = ua + u_pe(None, False, pairs=[(1, 0), (2, 0)])
                ub_ = u_pe(None, True, pairs=[(0, 1), (1, 1), (2, 1)])
                for jj in range(3):
                    u_cons(jj, ua + ub_)
                units = u_pe([3], False)
                u_cons(3, units)
            else:
                units = u_pe([0, 1], True)
                u_cons(0, units)
                u_cons(1, units)
                for jj in (2, 3):
                    units = u_pe([jj], False)
                    u_cons(jj, units)
            yield
            for gi in range(3):
                mark("L%d u%d+zm%d" % (l, gi + 1, gi))
                u_pe, u_cons = u_prep(gi + 1)
                zstep = zm_prep(gi)
                if gi == 0 and staggered:
                    seq = [("u", 0), ("u", 1), ("z", 0), ("u", 2), ("z", 1), ("u", 3), ("z", 2), ("z", 3)]
                else:
                    seq = [(kind, jj) for jj in range(4) for kind in ("u", "z")]
                for kind, jj in seq:
                    if kind == "u":
                        units = u_pe([jj], False)
                        u_cons(jj, units)
                    else:
                        zstep(jj)
                yield
            mark("L%d zm3" % l)
            spare = 6 + (1 - tile.hb)
            zstep = zm_prep(3, zrows=[4, 5, spare, 4])
            zstep.zpart(0)
            zstep.zpart(1)
            zstep.zpart(2)
            zstep.mpart(0)
            zstep.zpart(3)
            zstep.mpart(1)
            zstep.mpart(2)
            zstep.mpart(3)
            yield

        def out_proj(tile):
            xb, xb_b = tile.cx.xb, tile.cx.xb_b
            for mb in range(4):
                s = next_block("out")
                for m2 in range(2):
                    m = mb * 2 + m2
                    for si, (cs, n) in enumerate(tile.segs):
                        bk = next_bank()
                        for kc in range(NC_E):
                            g.mm(ps[:, bk, 0:n], wr[:, s, (kc * 2 + m2) * 128:(kc * 2 + m2 + 1) * 128],
                                 yv(tile, kc, cs, n), kc == 0, kc == NC_E - 1,
                                 (wr_b[s], yB(tile, kc)), (bank_b[bk],), flag=(kc == NC_E - 1))
                        g.tt(hs(tile, m, si, n), ps[:, bk, 0:n], hs(tile, m, si, n), ALU.add,
                             (bank_b[bk], hB(tile, m)), (hB(tile, m),))
                        g.act(xb[:, m, cs:cs + n], hs(tile, m, si, n), AF.Copy, (hB(tile, m),), (xb_b[m][si],))
                yield

        def gate_unit(tile, l, m, si, spg, spe):
            xb, xb_b = tile.cx.xb, tile.cx.xb_b
            rows, row_b = tile.cx.rows, tile.cx.row_b
            pb, pb_b, s_p = tile.cx.pb, tile.cx.pb_b, tile.cx.s_p
            HALF = tile.cx.HALF
            par = l % 2
            cs, n = tile.segs[si]
            u = tile.cx.unit
            tile.cx.unit = u + 1
            hp = u % 2
            ho = hp * HALF
            bg = next_bank()
            sl = spg[m // 4]
            m4 = m % 4
            for kc in range(NC_D):
                g.mm(ps[:, bg, 0:n], wr[:, sl, (kc * 4 + m4) * 128:(kc * 4 + m4 + 1) * 128],
                     xb[:, kc, cs:cs + n], kc == 0, kc == NC_D - 1,
                     (wr_b[sl], xb_b[kc][si]), (bank_b[bg],), flag=(kc == NC_D - 1))
            be = next_bank()
            for kc in range(2):
                g.mm(ps[:, be, 0:n], wr[:, spe, (kc * 8 + m) * 128:(kc * 8 + m + 1) * 128],
                     pb[:, par, kc, cs:cs + n], kc == 0, kc == 1,
                     (wr_b[spe], pb_b[par]), (bank_b[be],), flag=(kc == 1))
            gt = rows[:, 2, ho:ho + n]
            tv = rows[:, 3, ho:ho + n]
            g.act(gt, ps[:, bg, 0:n], AF.Sigmoid, (bank_b[bg],), (row_b[2][hp],))
            g.tt(tv, ps[:, be, 0:n], gt, ALU.mult, (bank_b[be], row_b[2][hp]), (row_b[3][hp],))
            g.tt(hs(tile, m, si, n), hs(tile, m, si, n), tv, ALU.add,
                 (hB(tile, m), row_b[3][hp]), (hB(tile, m),))

        def gate_staggered(tile, l):
            spg = [next_block("pg"), next_block("pg")]
            spe = next_block("pe")
            ssc = stats_begin(tile, 0)
            units = [(si, m) for si in range(2) for m in range(NC_D)]
            naq = []
            for idx, (si, m) in enumerate(units):
                gate_unit(tile, l, m, si, spg, spe)
                if idx >= 1:
                    a_si, a_m = units[idx - 1]
                    stats_act_seg(ssc, a_m, a_si)
                if idx >= 2:
                    b_si, b_m = units[idx - 2]
                    stats_pe_seg(ssc, b_m, b_si)
                    if (b_si, b_m) == (0, NC_D - 1):
                        stats_end_seg(ssc, 0)
                        naq = norm_apply_jobs(tile, l + 1, 0)
                if naq:
                    naq.pop(0)()
            while naq:
                naq.pop(0)()
            stats_act_seg(ssc, NC_D - 1, 1)
            stats_pe_seg(ssc, NC_D - 2, 1)

            def tail():
                stats_pe_seg(ssc, NC_D - 1, 1)
                stats_end_seg(ssc, 1)
                norm_apply_seg(tile, l + 1, 1)
            state["pending"].append(tail)

        def gate(tile, l, nxt):
            xb, xb_b = tile.cx.xb, tile.cx.xb_b
            rows, row_b = tile.cx.rows, tile.cx.row_b
            pb, pb_b, s_p = tile.cx.pb, tile.cx.pb_b, tile.cx.s_p
            HALF = tile.cx.HALF
            par = l % 2
            spg = [next_block("pg"), next_block("pg")]
            spe = next_block("pe")
            ssc = stats_begin(tile, 0)
            ssn = None
            for m in range(NC_D):
                if nxt is not None and m == 4:
                    ssn = stats_begin(nxt, 1)
                for si, (cs, n) in enumerate(tile.segs):
                    u = tile.cx.unit
                    tile.cx.unit = u + 1
                    hp = u % 2
                    ho = hp * HALF
                    bg = next_bank()
                    sl = spg[m // 4]
                    m4 = m % 4
                    for kc in range(NC_D):
                        g.mm(ps[:, bg, 0:n], wr[:, sl, (kc * 4 + m4) * 128:(kc * 4 + m4 + 1) * 128],
                             xb[:, kc, cs:cs + n], kc == 0, kc == NC_D - 1,
                             (wr_b[sl], xb_b[kc][si]), (bank_b[bg],), flag=(kc == NC_D - 1))
                    be = next_bank()
                    for kc in range(2):
                        g.mm(ps[:, be, 0:n], wr[:, spe, (kc * 8 + m) * 128:(kc * 8 + m + 1) * 128],
                             pb[:, par, kc, cs:cs + n], kc == 0, kc == 1,
                             (wr_b[spe], pb_b[par]), (bank_b[be],), flag=(kc == 1))
                    gt = rows[:, 2, ho:ho + n]
                    tv = rows[:, 3, ho:ho + n]
                    g.act(gt, ps[:, bg, 0:n], AF.Sigmoid, (bank_b[bg],), (row_b[2][hp],))
                    g.tt(tv, ps[:, be, 0:n], gt, ALU.mult, (bank_b[be], row_b[2][hp]), (row_b[3][hp],))
                    g.tt(hs(tile, m, si, n), hs(tile, m, si, n), tv, ALU.add,
                         (hB(tile, m), row_b[3][hp]), (hB(tile, m),))
                if m >= 1:
                    stats_act(ssc, m - 1)
                if m >= 2:
                    stats_pe(ssc, m - 2)
                if ssn is not None:
                    if m >= 5:
                        stats_pe(ssn, 2 * (m - 5))
                        stats_pe(ssn, 2 * (m - 5) + 1)
                    stats_act(ssn, 2 * (m - 4))
                    stats_act(ssn, 2 * (m - 4) + 1)
            stats_act(ssc, NC_D - 1)
            stats_pe(ssc, NC_D - 2)
            if ssn is not None:
                stats_pe(ssn, NC_D - 2)
                stats_pe(ssn, NC_D - 1)
            if tile.T == TP:
                keep_warm(WARM_A)
            stats_pe(ssc, NC_D - 1)
            if tile.T == TP:
                keep_warm(WARM_B)
            if ssn is not None:
                stats_end(ssn)
            stats_end(ssc)

        def final_job(tile, c):
            def job():
                for si, (cs, n) in enumerate(tile.segs):
                    g.stt(hs(tile, c, si, n), hs(tile, c, si, n), cc(C_FG + c), rr(tile, cs, n),
                          ALU.mult, ALU.mult, (hB(tile, c), cst_b) + rrB(tile), (hB(tile, c),))
                g.dma("sp", tile.y_dst[:, c], hfull(tile, c), s_yo[c], reads=(hB(tile, c),))
            return job

        def state_out_job(tile):
            def job():
                co, po = tile.st_out
                cp = tile.cpar
                g.dma("sp", co, cconv[:, cp, :, :, :], s_sto[0], reads=(cconv_b[cp][0], cconv_b[cp][1]))
                g.dma("sp", po, cpool[:, cp, :, :, :], s_sto[1], reads=(cpool_b[cp][0], cpool_b[cp][1]))
            return job

        ptiles = []
        for ti in range(NTILE_P):
            tl = Tile(TP, xp[ti], [pp[ti, l] for l in range(DEPTH)], yp[ti], ti % 2 == 0, ti % 2 == 1, True,
                      (cop[ti // 2], pop[ti // 2]))
            tl.hb = ti % 2
            tl.cpar = (ti // 2) % 2
            tl.cx = cxp
            ptiles.append(tl)
        smp = Tile(TS, xs, [psm[l] for l in range(DEPTH)], ys, True, True, False, (cos, pos))
        smp.hb = 0
        smp.cpar = 0
        smp.cx = cxs

        def run(gm, gc):
            if gc is None:
                for _ in gm:
                    pass
                return
            state["logging"] = True
            for _ in gm:
                state["replay"] = True
                while state["log"]:
                    next(gc)
                state["replay"] = False
            state["logging"] = False
            for _ in gc:
                raise AssertionError("co-tile generator out of sync")

        def mixer(tile, l, stg):
            return mixer_a(tile, l, stg) if l % 2 == 0 else mixer_b(tile, l, stg)

        def prep_tile(tile):
            for c in range(NC_D):
                load_x(tile, c)
            init_carries(tile)
            load_p(tile, 0)
            ss0 = stats_begin(tile, 0)
            for c in range(NC_D):
                stats_act(ss0, c)
                stats_pe(ss0, c)
            stats_end(ss0)
            norm_apply(tile, 0)

        prep_tile(ptiles[0])
        for i, tile in enumerate(ptiles):
            nxt = ptiles[i + 1] if i + 1 < len(ptiles) else None
            co = smp if i == len(ptiles) - 1 else None
            mark("tile start")
            for l in range(DEPTH):
                stg = (l > 0) and stag(tile)
                if l > 0 and not stg:
                    norm_apply(tile, l)
                if co is not None and l > 0:
                    norm_apply(co, l)
                mark("L%d mixer" % l)
                run(mixer(tile, l, stg), mixer(co, l, False) if co is not None else None)
                assert not state["pending"]
                flush_side()
                if i == len(ptiles) - 2 and l == 0:
                    prep_tile(smp)
                if l + 1 < DEPTH:
                    load_p(tile, l + 1)
                    if co is not None:
                        load_p(co, l + 1)
                elif nxt is not None:
                    load_p(nxt, 0)
                mark("L%d out" % l)
                run(out_proj(tile), out_proj(co) if co is not None else None)
                last_l = (l == DEPTH - 1)
                if last_l and nxt is not None:
                    for c in range(NC_D):
                        load_x(nxt, c)
                    if nxt.first:
                        init_carries(nxt)
                mark("L%d gate" % l)
                state["logging"] = co is not None
                if (not last_l) and stag(tile):
                    gate_staggered(tile, l)
                else:
                    gate(tile, l, nxt if last_l else None)
                state["logging"] = False
                if co is not None:
                    state["replay"] = True
                    gate(co, l, None)
                    state["replay"] = False
                    assert not state["log"]
            mark("final")
            jobs = [final_job(tile, c) for c in range(NC_D)]
            if tile.last:
                jobs.append(state_out_job(tile))
            if nxt is not None:
                norm_apply_seg(nxt, 0, 0)
                jobs.pop(0)()
                jobs.pop(0)()
                norm_apply_seg(nxt, 0, 1)
                state["side"] = jobs
            else:
                for jb in jobs:
                    jb()
            if co is not None:
                for c in range(NC_D):
                    final_job(co, c)()
                state_out_job(co)()
        g.finish("sp")

        @block.tensor
        def _(t):
            g.replay("pe", t)

        @block.scalar
        def _(a):
            g.replay("act", a)

        @block.vector
        def _(v):
            g.replay("dve", v)

        @block.gpsimd
        def _(p):
            g.replay("pool", p)

        @block.sync
        def _(s):
            g.replay("sp", s)

    return nc


_CACHE = {}
MARKS = []


def _kc_split(W):
    K, M = W.shape
    return W.reshape(K // 128, 128, M).transpose(1, 0, 2)


def build_wstream(w_in_a, w_out_a, w_in_b, w_grp_b, w_out_b, w_pe, w_pg):
    out = np.empty((128, WTOT), dtype=np.float32)
    for bi, (kind, l, idx, n) in enumerate(SCHED):
        k = l // 2
        if kind == "ain":
            W = _kc_split(w_in_a[k]).reshape(128, 8, 4, 16, 128)
            blk = W[:, :, :, idx, :]
        elif kind == "bu":
            W = _kc_split(w_in_b[k]).reshape(128, 8, 2, 4, 4, 128)
            blk = W[:, :, 0, idx]
        elif kind == "bz":
            W = _kc_split(w_in_b[k]).reshape(128, 8, 2, 4, 4, 128)
            blk = W[:, :, 1, idx]
        elif kind == "bg":
            blk = _kc_split(w_grp_b[k][idx])
        elif kind == "out":
            Wo = w_out_a[k] if l % 2 == 0 else w_out_b[k]
            blk = _kc_split(Wo).reshape(128, 16, 4, 2, 128)[:, :, idx]
        elif kind == "pg":
            blk = _kc_split(w_pg[l]).reshape(128, 8, 2, 4, 128)[:, :, idx]
        elif kind == "pe":
            blk = _kc_split(w_pe[l])
        o = int(WOFF[bi])
        out[:, o:o + n] = np.ascontiguousarray(blk).reshape(128, n)
    return out


def kernel(x_prompt, x_sample, state_conv, state_pool, p_prompt, p_sample, norm_g,
           w_in_a, conv_w_a, w_out_a, w_in_b, w_grp_b, scale_b, w_out_b, w_pe, w_pg, final_g):
    f = lambda a: np.asarray(a, dtype=np.float32)
    x_prompt, x_sample, state_conv, state_pool = f(x_prompt), f(x_sample), f(state_conv), f(state_pool)
    p_prompt, p_sample = f(p_prompt), f(p_sample)
    norm_g, conv_w_a, scale_b, final_g = f(norm_g), f(conv_w_a), f(scale_b), f(final_g)

    if "nc" not in _CACHE:
        _CACHE["nc"] = build_program()
    nc = _CACHE["nc"]

    wst = build_wstream(f(w_in_a), f(w_out_a), f(w_in_b), f(w_grp_b), f(w_out_b), f(w_pe), f(w_pg))
    cst = np.zeros((128, NCONST), np.float32)
    cst[:, C_G:C_G + 32] = norm_g.reshape(4, 8, 128).transpose(2, 0, 1).reshape(128, 32)
    cst[:, C_FG:C_FG + 8] = final_g.reshape(8, 128).T
    cst[:, C_CW:C_CW + 96] = conv_w_a.reshape(2, 3, 16, 128).transpose(3, 0, 1, 2).reshape(128, 96)
    cst[:, C_SC:C_SC + 32] = scale_b.reshape(2, 16, 128).transpose(2, 0, 1).reshape(128, 32)
    cst[:, C_RC:C_RC + 16] = (1.0 / np.arange(1, 17, dtype=np.float64)).astype(np.float32)[None, :]

    in_maps = []
    for i in range(NCORE):
        xb_ = x_prompt[4 * i:4 * i + 4]
        xpi = xb_.reshape(4, 2, TP, NC_D, 128).transpose(0, 1, 4, 3, 2).reshape(NTILE_P, 128, NC_D, TP)
        ppi = p_prompt[:, 4 * i:4 * i + 4].reshape(DEPTH, 4, 2, TP, 2, 128).transpose(1, 2, 0, 5, 4, 3)
        ppi = ppi.reshape(NTILE_P, DEPTH, 128, 2, TP)
        xsi = x_sample[i].reshape(TS, NC_D, 128).transpose(2, 1, 0)
        psi = p_sample[:, i].reshape(DEPTH, TS, 2, 128).transpose(0, 3, 2, 1)
        sci = state_conv[:, i].reshape(2, 2, NC_E, 128).transpose(3, 0, 2, 1)
        spi = state_pool[:, i].reshape(2, 15, NC_E, 128).transpose(3, 0, 2, 1)
        in_maps.append({
            "xp": np.ascontiguousarray(xpi).reshape(NTILE_P, 128, NC_D, 2, 512), "xs": np.ascontiguousarray(xsi),
            "pp": np.ascontiguousarray(ppi), "psm": np.ascontiguousarray(psi),
            "sci": np.ascontiguousarray(sci), "spi": np.ascontiguousarray(spi),
            "cst": cst, "wst": wst,
        })

    res = run_bass_kernel_spmd(nc, in_maps, core_ids=list(range(NCORE)))
    R = res.results

    y_prompt = np.empty((32, SEQ, D), np.float32)
    y_sample = np.empty((8, TS, D), np.float32)
    conv_p = np.empty((2, 32, 2, E), np.float32)
    pool_p = np.empty((2, 32, 15, E), np.float32)
    conv_s = np.empty((2, 8, 2, E), np.float32)
    pool_s = np.empty((2, 8, 15, E), np.float32)
    for i in range(NCORE):
        r = R[i]
        ypi = np.asarray(r["yp"]).reshape(4, 2, 128, NC_D, TP).transpose(0, 1, 4, 3, 2).reshape(4, SEQ, D)
        y_prompt[4 * i:4 * i + 4] = ypi
        y_sample[i] = np.asarray(r["ys"]).transpose(2, 1, 0).reshape(TS, D)
        cop = np.asarray(r["cop"])
        conv_p[:, 4 * i:4 * i + 4] = cop.transpose(2, 0, 4, 3, 1).reshape(2, 4, 2, E)
        pop = np.asarray(r["pop"])
        pool_p[:, 4 * i:4 * i + 4] = pop.transpose(2, 0, 4, 3, 1).reshape(2, 4, 15, E)
        conv_s[:, i] = np.asarray(r["cos"]).transpose(1, 3, 2, 0).reshape(2, 2, E)
        pool_s[:, i] = np.asarray(r["pos"]).transpose(1, 3, 2, 0).reshape(2, 15, E)
    return (y_prompt, y_sample, conv_p, pool_p, conv_s, pool_s)
```

```python
import numpy as np
import concourse.bass as bass
import concourse.mybir as mybir
from concourse.bass_utils import run_bass_kernel_spmd

F32 = mybir.dt.float32
BF16 = mybir.dt.bfloat16
AF = mybir.ActivationFunctionType
ALU = mybir.AluOpType

NCORE = 8
D = 1024
NC_D = 8
E = 2048
NC_E = 16
DEPTH = 4
SEQ = 2048
TP = 1024
TS = 32
NTILE_P = 8
EPS = 1e-6
WINDOWS = (2, 4, 8, 16)
NSLOT = 5
WBLK = 4096
ROWW = 1040
HALF = 520
NROW = 8
STAGGER = True
WARM_A = 10
WARM_B = 18

C_G = 0
C_FG = 32
C_CW = 40
C_SC = 136
C_RC = 168
NCONST = 184


def block_schedule():
    bl = []
    for l in range(DEPTH):
        if l % 2 == 0:
            for j in range(NC_E):
                bl.append(("ain", l, j, 4096))
        else:
            for kind, g in (("bu", 0), ("bu", 1), ("bz", 0), ("bg", 0), ("bu", 2), ("bz", 1),
                            ("bg", 1), ("bu", 3), ("bz", 2), ("bg", 2), ("bz", 3), ("bg", 3)):
                bl.append((kind, l, g, 2048 if kind == "bg" else 4096))
        for mb in range(4):
            bl.append(("out", l, mb, 4096))
        for mb in range(2):
            bl.append(("pg", l, mb, 4096))
        bl.append(("pe", l, 0, 2048))
    return bl


SCHED = block_schedule()
WOFF = np.concatenate([[0], np.cumsum([b[3] for b in SCHED])]).astype(np.int64)
WTOT = int(WOFF[-1])


class Sem:
    __slots__ = ("h", "n")

    def __init__(self, h):
        self.h = h
        self.n = 0


class Buf:
    __slots__ = ("w", "r")

    def __init__(self):
        self.w = None
        self.r = []


class Gen:
    ENG = ("pe", "act", "dve", "pool", "sp")

    def __init__(self, sem_handles):
        self.free_sems = list(sem_handles)
        self.ops = {e: [] for e in self.ENG}
        self.seen = {e: {} for e in self.ENG}
        self.prog = {e: self.new_sem() for e in ("pe", "act", "dve", "pool")}
        self.dma_sems = []

    def new_sem(self):
        return Sem(self.free_sems.pop(0))

    def new_dma_sem(self):
        s = self.new_sem()
        self.dma_sems.append(s)
        return s

    def _wait(self, e, ev):
        s, v = ev
        own = self.prog.get(e)
        if s is own:
            if e == "pe":
                return
            if v < s.n - 1:
                return
        if self.seen[e].get(s, 0) >= v:
            return
        self.seen[e][s] = v
        self.ops[e].append(("wait", s, v))

    def _deps(self, e, reads, writes):
        for b in reads:
            if b.w is not None:
                self._wait(e, b.w)
        for b in writes:
            if b.w is not None:
                self._wait(e, b.w)
            for ev in b.r:
                self._wait(e, ev)

    @staticmethod
    def _mark(ev, reads, writes):
        s = ev[0]
        for b in reads:
            b.r = [x for x in b.r if x[0] is not s] + [ev]
        for b in writes:
            b.w = ev
            b.r = []

    def emit(self, e, fn, reads=(), writes=(), inc=True):
        self._deps(e, reads, writes)
        s = self.prog[e]
        if inc:
            s.n += 1
            ev = (s, s.n)
            self.ops[e].append(("ins", fn, s, 1))
        else:
            ev = (s, s.n + 1)
            self.ops[e].append(("ins", fn, None, 0))
        self._mark(ev, reads, writes)

    def dma(self, q, out_ap, in_ap, sem, reads=(), writes=()):
        self._deps(q, reads, writes)
        if sem.n > 0:
            self._wait(q, (sem, sem.n))
        sem.n += 16
        ev = (sem, sem.n)
        self.ops[q].append(("ins", (lambda eng: eng.dma_start(out=out_ap, in_=in_ap)), sem, 16))
        self._mark(ev, reads, writes)

    def finish(self, q="sp"):
        for s in self.dma_sems:
            if s.n > 0:
                self._wait(q, (s, s.n))

    def replay(self, e, eng):
        for op in self.ops[e]:
            if op[0] == "wait":
                eng.wait_ge(op[1].h, op[2])
            else:
                ins = op[1](eng)
                if op[2] is not None:
                    ins.then_inc(op[2].h, op[3])

    def mm(self, out, lhsT, rhs, start, stop, reads, writes, flag):
        self.emit("pe", (lambda t: t.matmul(out, lhsT, rhs, start=start, stop=stop)),
                  reads, writes, inc=flag)

    def act(self, out, in_, func, reads, writes, scale=None, bias=None):
        kw = {}
        if scale is not None:
            kw["scale"] = scale
        if bias is not None:
            kw["bias"] = bias
        self.emit("act", (lambda a: a.activation(out=out, in_=in_, func=func, **kw)), reads, writes)

    def tt(self, out, in0, in1, op, reads, writes):
        self.emit("dve", (lambda v: v.tensor_tensor(out=out, in0=in0, in1=in1, op=op)), reads, writes)

    def stt(self, out, in0, scalar, in1, op0, op1, reads, writes):
        self.emit("dve", (lambda v: v.scalar_tensor_tensor(out=out, in0=in0, scalar=scalar, in1=in1,
                                                           op0=op0, op1=op1)), reads, writes)

    def recip(self, out, in_, reads, writes):
        self.emit("dve", (lambda v: v.reciprocal(out=out, in_=in_)), reads, writes)

    def memset(self, e, ap, val, writes):
        self.emit(e, (lambda v: v.memset(ap, val)), (), writes)


class Tile:
    def __init__(self, T, x_src, p_src, y_dst, first, last, is_prompt, st_out):
        self.T = T
        self.x_src = x_src
        self.p_src = p_src
        self.y_dst = y_dst
        self.first = first
        self.last = last
        self.is_prompt = is_prompt
        self.st_out = st_out
        self.segs = [(s, min(512, T - s)) for s in range(0, T, 512)]


def build_program():
    nc = bass.Bass("TRN2", target_bir_lowering=False)
    dt = nc.dram_tensor
    xp = dt("xp", [NTILE_P, 128, NC_D, 2, 512], F32, kind="ExternalInput").ap()
    xs = dt("xs", [128, NC_D, TS], F32, kind="ExternalInput").ap()
    pp = dt("pp", [NTILE_P, DEPTH, 128, 2, TP], F32, kind="ExternalInput").ap()
    psm = dt("psm", [DEPTH, 128, 2, TS], F32, kind="ExternalInput").ap()
    sci = dt("sci", [128, 2, NC_E, 2], F32, kind="ExternalInput").ap()
    spi = dt("spi", [128, 2, NC_E, 15], F32, kind="ExternalInput").ap()
    cstd = dt("cst", [128, NCONST], F32, kind="ExternalInput").ap()
    wst = dt("wst", [128, WTOT], F32, kind="ExternalInput").ap()
    yp = dt("yp", [NTILE_P, 128, NC_D, 2, 512], F32, kind="ExternalOutput").ap()
    ys = dt("ys", [128, NC_D, TS], F32, kind="ExternalOutput").ap()
    cop = dt("cop", [4, 128, 2, NC_E, 2], F32, kind="ExternalOutput").ap()
    pop = dt("pop", [4, 128, 2, NC_E, 15], F32, kind="ExternalOutput").ap()
    cos = dt("cos", [128, 2, NC_E, 2], F32, kind="ExternalOutput").ap()
    pos = dt("pos", [128, 2, NC_E, 15], F32, kind="ExternalOutput").ap()
    wcache = dt("wcache", [128, WTOT], BF16, kind="Internal").ap()

    import contextlib
    with contextlib.ExitStack() as es:
        sb = lambda name, shape, dtp: es.enter_context(nc.sbuf_tensor(name, shape, dtp))
        BUF = [sb("bufA", [128, NC_E, TP], BF16), sb("bufB", [128, NC_E, TP], BF16)]
        Hf = [b_.bitcast(F32) for b_ in BUF]
        xb = sb("xb", [128, NC_D, TP], BF16)
        pb = sb("pb", [128, 2, 2, TP], BF16)
        wr = sb("wr", [128, NSLOT, WBLK], BF16)
        sq = sb("sq", [128, 4, 2, 512], BF16)
        rows = sb("rows", [128, NROW, ROWW], F32)
        dbuf = sb("dbuf", [128, 2, 4, TP], BF16)
        cst = sb("cstt", [128, NCONST], F32)
        cconv = sb("cconv", [128, 2, 2, NC_E, 2], F32)
        cpool = sb("cpool", [128, 2, 2, NC_E, 15], F32)
        ones = sb("ones", [128, 128], BF16)
        epsb = sb("epsb", [128, 1], F32)
        fix = sb("fix", [128, 16], F32)
        junk = sb("junk", [128, 512], BF16)
        SW = 64
        BUF_s = [sb("bufAs", [128, NC_E, SW], BF16), sb("bufBs", [128, NC_E, SW], BF16)]
        Hf_s = [b_.bitcast(F32) for b_ in BUF_s]
        xb_s = sb("xbs", [128, NC_D, SW], BF16)
        pb_s = sb("pbs", [128, 2, 2, SW], BF16)
        sq_s = sb("sqs", [128, 4, 2, SW], BF16)
        rows_s = sb("rowss", [128, NROW, 2 * SW], F32)
        dbuf_s = sb("dbufs", [128, 2, 4, SW], BF16)
        fix_s = sb("fixs", [128, 16], F32)
        ps = es.enter_context(nc.psum_tensor("ps", [128, 8, 512], F32))
        sem_handles = [es.enter_context(nc.semaphore("s%d" % i)) for i in range(40)]
        block = es.enter_context(nc.Block())

        g = Gen(sem_handles)
        P = [[Buf() for _ in range(NC_D)] for _ in range(2)]
        xb_b = [[Buf(), Buf()] for _ in range(NC_D)]
        pb_b = [Buf(), Buf()]
        wr_b = [Buf() for _ in range(NSLOT)]
        sq_b = [Buf() for _ in range(4)]
        row_b = [[Buf(), Buf()] for _ in range(NROW)]
        db_b = [Buf(), Buf()]
        cst_b = Buf()
        cconv_b = [[Buf(), Buf()], [Buf(), Buf()]]
        cpool_b = [[Buf(), Buf()], [Buf(), Buf()]]
        ones_b = Buf()
        eps_b = Buf()
        fix_b = Buf()
        junk_b = Buf()
        bank_b = [Buf() for _ in range(8)]
        s_w = [g.new_dma_sem() for _ in range(NSLOT)]
        s_x = [g.new_dma_sem() for _ in range(NC_D)]
        s_p = [g.new_dma_sem(), g.new_dma_sem()]
        s_yo = [g.new_dma_sem() for _ in range(NC_D)]
        s_c = g.new_dma_sem()
        s_sti = [g.new_dma_sem(), g.new_dma_sem()]
        s_sto = [g.new_dma_sem(), g.new_dma_sem()]
        s_wb = [g.new_dma_sem() for _ in range(NSLOT)]
        cache_b = [Buf() for _ in range(NSLOT)]

        class Cx:
            pass

        cxp = Cx()
        cxp.BUF, cxp.Hf, cxp.P = BUF, Hf, P
        cxp.xb, cxp.xb_b, cxp.pb, cxp.pb_b, cxp.s_p = xb, xb_b, pb, pb_b, s_p
        cxp.sq, cxp.sq_b, cxp.rows, cxp.row_b = sq, sq_b, rows, row_b
        cxp.dbuf, cxp.db_b, cxp.fix, cxp.fix_b = dbuf, db_b, fix, fix_b
        cxp.HALF, cxp.unit = HALF, 0
        cxs = Cx()
        cxs.BUF, cxs.Hf = BUF_s, Hf_s
        cxs.P = [[Buf() for _ in range(NC_D)] for _ in range(2)]
        cxs.xb, cxs.xb_b = xb_s, [[Buf(), Buf()] for _ in range(NC_D)]
        cxs.pb, cxs.pb_b, cxs.s_p = pb_s, [Buf(), Buf()], [g.new_dma_sem(), g.new_dma_sem()]
        cxs.sq, cxs.sq_b = sq_s, [Buf() for _ in range(4)]
        cxs.rows, cxs.row_b = rows_s, [[Buf(), Buf()] for _ in range(NROW)]
        cxs.dbuf, cxs.db_b, cxs.fix, cxs.fix_b = dbuf_s, [Buf(), Buf()], fix_s, Buf()
        cxs.HALF, cxs.unit = SW, 0

        state = {"bank": 0, "blk": 0, "hold": set(), "side": [], "pending": [],
                 "log": [], "logging": False, "replay": False}

        def mark(label):
            MARKS.append((label, g.prog["pe"].n, g.prog["act"].n, g.prog["dve"].n,
                          sum(1 for o in g.ops["pe"] if o[0] == "ins")))

        def next_bank():
            b = state["bank"]
            while b in state["hold"]:
                b = (b + 1) % 8
            state["bank"] = (b + 1) % 8
            return b

        def next_block(kind):
            if state["replay"]:
                k_, s_ = state["log"].pop(0)
                assert k_ == kind, (k_, kind)
                return s_
            i = state["blk"]
            state["blk"] = i + 1
            bi = i % len(SCHED)
            assert SCHED[bi][0] == kind, (SCHED[bi], kind)
            s = i % NSLOT
            n = SCHED[bi][3]
            off = int(WOFF[bi])
            if i < len(SCHED):
                first_rd = tuple(P[0][c] for c in range(NC_D)) if i == 0 else ()
                g.dma("pool", wr[:, s, 0:n], wst[:, off:off + n], s_w[s], reads=first_rd, writes=(wr_b[s],))
                g.dma("sp", wcache[:, off:off + n], wr[:, s, 0:n], s_wb[s], reads=(wr_b[s],), writes=(cache_b[s],))
            else:
                g.dma("pool", wr[:, s, 0:n], wcache[:, off:off + n], s_w[s], reads=(cache_b[bi % NSLOT],),
                      writes=(wr_b[s],))
            if state["logging"]:
                state["log"].append((kind, s))
            return s

        def cc(idx):
            return cst[:, idx:idx + 1]

        def hs(tile, c, si, n):
            return tile.cx.Hf[tile.hb][:, 2 * c + si, 0:n]

        def hfull(tile, c):
            if tile.T == TP:
                return tile.cx.Hf[tile.hb][:, 2 * c:2 * c + 2, :]
            return tile.cx.Hf[tile.hb][:, 2 * c, 0:tile.T]

        def yv(tile, j, cs, n):
            return tile.cx.BUF[1 - tile.hb][:, j, cs:cs + n]

        def hB(tile, c):
            return tile.cx.P[tile.hb][c]

        def yB(tile, j):
            return tile.cx.P[1 - tile.hb][j // 2]

        def rr(tile, cs, n):
            return tile.cx.rows[:, 6 + tile.hb, cs:cs + n]

        def rrB(tile):
            return (tile.cx.row_b[6 + tile.hb][0], tile.cx.row_b[6 + tile.hb][1])

        g.dma("sp", cst[:, :], cstd, s_c, writes=(cst_b,))
        g.memset("dve", ones[:, :], 1.0, (ones_b,))
        g.memset("dve", epsb[:, :], EPS, (eps_b,))
        g.memset("dve", junk[:, :], 1.0, (junk_b,))
        g.memset("dve", rows[:, :, :], 0.0, [b for r in row_b for b in r])
        g.memset("dve", rows_s[:, :, :], 0.0, [b for r in cxs.row_b for b in r])

        def load_p(tile, l):
            pb, pb_b, s_p = tile.cx.pb, tile.cx.pb_b, tile.cx.s_p
            par = l % 2
            g.dma("pool", pb[:, par, :, 0:tile.T], tile.p_src[l], s_p[par], writes=(pb_b[par],))

        def load_x(tile, c):
            g.dma("sp", hfull(tile, c), tile.x_src[:, c], s_x[c], writes=(hB(tile, c),))

        def init_carries(tile):
            cp = tile.cpar
            if tile.is_prompt:
                for k in range(2):
                    g.memset("dve", cconv[:, cp, k, :, :], 0.0, (cconv_b[cp][k],))
                    g.memset("dve", cpool[:, cp, k, :, :], 0.0, (cpool_b[cp][k],))
            else:
                g.dma("sp", cconv[:, cp, :, :, :], sci, s_sti[0], writes=(cconv_b[cp][0], cconv_b[cp][1]))
                g.dma("sp", cpool[:, cp, :, :, :], spi, s_sti[1], writes=(cpool_b[cp][0], cpool_b[cp][1]))

        def stats_begin(tile, sset):
            bks = [next_bank() for _ in tile.segs]
            state["hold"] |= set(bks)
            return {"tile": tile, "banks": bks, "set": sset}

        def sq_idx(ss, c):
            return ss["set"] * 2 + (c % 2)

        def stats_act(ss, c):
            tile = ss["tile"]
            sq, sq_b = tile.cx.sq, tile.cx.sq_b
            qi = sq_idx(ss, c)
            out = sq[:, qi, :, :] if tile.T == TP else sq[:, qi, 0, 0:tile.T]
            g.act(out, hfull(tile, c), AF.Square, (hB(tile, c),), (sq_b[qi],))

        def stats_pe(ss, c):
            tile = ss["tile"]
            sq, sq_b = tile.cx.sq, tile.cx.sq_b
            qi = sq_idx(ss, c)
            for si, (cs, n) in enumerate(tile.segs):
                bk = ss["banks"][si]
                g.mm(ps[:, bk, 0:n], ones[:, :], sq[:, qi, si, 0:n], c == 0, c == NC_D - 1,
                     (ones_b, sq_b[qi]), (bank_b[bk],), flag=True)

        def stats_act_seg(ss, c, si):
            tile = ss["tile"]
            sq, sq_b = tile.cx.sq, tile.cx.sq_b
            qi = sq_idx(ss, c)
            cs, n = tile.segs[si]
            g.act(sq[:, qi, si, 0:n], hs(tile, c, si, n), AF.Square, (hB(tile, c),), (sq_b[qi],))

        def stats_pe_seg(ss, c, si):
            tile = ss["tile"]
            sq, sq_b = tile.cx.sq, tile.cx.sq_b
            qi = sq_idx(ss, c)
            cs, n = tile.segs[si]
            bk = ss["banks"][si]
            g.mm(ps[:, bk, 0:n], ones[:, :], sq[:, qi, si, 0:n], c == 0, c == NC_D - 1,
                 (ones_b, sq_b[qi]), (bank_b[bk],), flag=True)

        def stats_end_seg(ss, si):
            tile = ss["tile"]
            rows, row_b = tile.cx.rows, tile.cx.row_b
            r4 = (row_b[4][0], row_b[4][1])
            cs, n = tile.segs[si]
            bk = ss["banks"][si]
            g.act(rows[:, 4, cs:cs + n], ps[:, bk, 0:n], AF.Ln, (bank_b[bk], eps_b), r4,
                  scale=1.0 / D, bias=epsb[:, 0:1])
            g.act(rr(tile, cs, n), rows[:, 4, cs:cs + n], AF.Exp, r4, rrB(tile), scale=-0.5)
            state["hold"] -= {bk}

        def run_pending(nmax=None):
            jobs = state["pending"]
            k = len(jobs) if nmax is None else min(nmax, len(jobs))
            state["pending"] = jobs[k:]
            for jb in jobs[:k]:
                jb()

        def stag(tile):
            return STAGGER and len(tile.segs) == 2

        def stats_end(ss):
            tile = ss["tile"]
            rows, row_b = tile.cx.rows, tile.cx.row_b
            r4 = (row_b[4][0], row_b[4][1])
            for si, (cs, n) in enumerate(tile.segs):
                bk = ss["banks"][si]
                g.act(rows[:, 4, cs:cs + n], ps[:, bk, 0:n], AF.Ln, (bank_b[bk], eps_b),
                      r4, scale=1.0 / D, bias=epsb[:, 0:1])
            for si, (cs, n) in enumerate(tile.segs):
                g.act(rr(tile, cs, n), rows[:, 4, cs:cs + n], AF.Exp, r4, rrB(tile), scale=-0.5)
            state["hold"] -= set(ss["banks"])

        def norm_apply_seg(tile, l, si):
            xb, xb_b = tile.cx.xb, tile.cx.xb_b
            if si >= len(tile.segs):
                return
            cs, n = tile.segs[si]
            for c in range(NC_D):
                g.stt(xb[:, c, cs:cs + n], hs(tile, c, si, n), cc(C_G + l * 8 + c), rr(tile, cs, n),
                      ALU.mult, ALU.mult, (hB(tile, c), cst_b) + rrB(tile), (xb_b[c][si],))

        def norm_apply(tile, l):
            for si in range(len(tile.segs)):
                norm_apply_seg(tile, l, si)

        def norm_apply_jobs(tile, l, si):
            xb, xb_b = tile.cx.xb, tile.cx.xb_b
            cs, n = tile.segs[si]

            def mk(c):
                return lambda: g.stt(xb[:, c, cs:cs + n], hs(tile, c, si, n), cc(C_G + l * 8 + c), rr(tile, cs, n),
                                     ALU.mult, ALU.mult, (hB(tile, c), cst_b) + rrB(tile), (xb_b[c][si],))
            return [mk(c) for c in range(NC_D)]

        def keep_warm(nmm):
            if nmm <= 0:
                return
            bk = next_bank()
            for i in range(nmm):
                g.mm(ps[:, bk, 0:512], ones[:, :], junk[:, :], True, True, (ones_b, junk_b), (bank_b[bk],),
                     flag=(i == nmm - 1))

        def pop_side():
            if state["side"]:
                state["side"].pop(0)()

        def flush_side():
            while state["side"]:
                state["side"].pop(0)()

        def mixer_a(tile, l, staggered):
            xb, xb_b = tile.cx.xb, tile.cx.xb_b
            rows, row_b = tile.cx.rows, tile.cx.row_b
            HALF = tile.cx.HALF
            k = l // 2
            T = tile.T
            cp = tile.cpar

            def hist(j):
                vr = j % 2
                g.act(rows[:, vr, 0:2], cconv[:, cp, k, j, :], AF.Copy, (cconv_b[cp][k],),
                      (row_b[vr][0], row_b[vr][1]))

            def carry(j):
                vr = j % 2
                g.act(cconv[:, cp, k, j, :], rows[:, vr, T:T + 2], AF.Copy, (row_b[vr][0], row_b[vr][1]),
                      (cconv_b[cp][k],))

            def unit_a(j, si, s, kc_outer, hook=None):
                cs, n = tile.segs[si]
                vr = j % 2
                vbufs = (row_b[vr][0], row_b[vr][1])
                u = tile.cx.unit
                tile.cx.unit = u + 1
                hp = u % 2
                ho = hp * HALF
                bks = [next_bank() for _ in range(4)]
                if kc_outer:
                    order = [(wi, kc) for kc in range(NC_D) for wi in range(4)]
                else:
                    order = [(wi, kc) for wi in range(4) for kc in range(NC_D)]
                for oi, (wi, kc) in enumerate(order):
                    if hook is not None and oi == 2 * NC_D:
                        hook()
                    g.mm(ps[:, bks[wi], 0:n], wr[:, s, (kc * 4 + wi) * 128:(kc * 4 + wi + 1) * 128],
                         xb[:, kc, cs:cs + n], kc == 0, kc == NC_D - 1,
                         (wr_b[s], xb_b[kc][si]), (bank_b[bks[wi]],), flag=(kc == NC_D - 1))
                bb, bc, bh, bz = bks
                hh = rows[:, 2, ho:ho + n]
                szv = rows[:, 3, ho:ho + n]
                acc = rows[:, 4, ho:ho + n]
                gz = rows[:, 5, ho:ho + n]
                g.act(hh, ps[:, bh, 0:n], AF.Copy, (bank_b[bh],), (row_b[2][hp],))
                g.tt(rows[:, vr, 2 + cs:2 + cs + n], ps[:, bc, 0:n], hh, ALU.mult,
                     (bank_b[bc], row_b[2][hp]), vbufs)
                g.act(szv, ps[:, bz, 0:n], AF.Silu, (bank_b[bz],), (row_b[3][hp],))
                g.tt(gz, ps[:, bb, 0:n], szv, ALU.mult, (bank_b[bb], row_b[3][hp]), (row_b[5][hp],))
                cw = C_CW + (k * 3) * 16 + j
                g.act(acc, rows[:, vr, cs:cs + n], AF.Copy, vbufs + (cst_b,), (row_b[4][hp],), scale=cc(cw))
                g.stt(acc, rows[:, vr, cs + 1:cs + 1 + n], cc(cw + 16), acc, ALU.mult, ALU.add,
                      vbufs + (cst_b, row_b[4][hp]), (row_b[4][hp],))
                g.stt(acc, rows[:, vr, cs + 2:cs + 2 + n], cc(cw + 32), acc, ALU.mult, ALU.add,
                      vbufs + (cst_b, row_b[4][hp]), (row_b[4][hp],))
                g.tt(yv(tile, j, cs, n), acc, gz, ALU.mult, (row_b[4][hp], row_b[5][hp]), (yB(tile, j),))
                pop_side()

            j0 = 0
            if staggered or len(tile.segs) == 2:
                s0 = next_block("ain")
                hist(0)
                s1 = next_block("ain")
                hist(1)
                unit_a(0, 0, s0, not staggered, hook=(lambda: run_pending(1)) if staggered else None)
                run_pending()
                unit_a(1, 0, s1, False)
                unit_a(0, 1, s0, True)
                unit_a(1, 1, s1, False)
                carry(0)
                carry(1)
                j0 = 2
                yield
            for j in range(j0, NC_E):
                s = next_block("ain")
                hist(j)
                for si in range(len(tile.segs)):
                    unit_a(j, si, s, (not staggered) and j == 0 and si == 0)
                carry(j)
                yield

        def mixer_b(tile, l, staggered):
            xb, xb_b = tile.cx.xb, tile.cx.xb_b
            rows, row_b = tile.cx.rows, tile.cx.row_b
            dbuf, db_b, fix, fix_b = tile.cx.dbuf, tile.cx.db_b, tile.cx.fix, tile.cx.fix_b
            k = l // 2
            T = tile.T
            EE = 16 + T
            cp = tile.cpar

            def u_prep(gi):
                w = WINDOWS[gi]
                s = next_block("bu")
                dp = gi % 2
                sa = (row_b[2][0], row_b[2][1])
                sbb = (row_b[3][0], row_b[3][1])

                def u_pe(jjs, kc_outer, pairs=None):
                    if pairs is None:
                        pairs = [(jj, si) for jj in jjs for si in range(len(tile.segs))]
                    units = [(jj, tile.segs[si][0], tile.segs[si][1], next_bank(), si) for (jj, si) in pairs]
                    if kc_outer:
                        order = [(ui, kc) for si_ in range(len(tile.segs)) for kc in range(NC_D)
                                 for ui in range(len(units)) if units[ui][4] == si_]
                    else:
                        order = [(ui, kc) for ui in range(len(units)) for kc in range(NC_D)]
                    for ui, kc in order:
                        jj, cs, n, bk, si = units[ui]
                        g.mm(ps[:, bk, 0:n], wr[:, s, (kc * 4 + jj) * 128:(kc * 4 + jj + 1) * 128],
                             xb[:, kc, cs:cs + n], kc == 0, kc == NC_D - 1,
                             (wr_b[s], xb_b[kc][si]), (bank_b[bk],), flag=(kc == NC_D - 1))
                    return units

                def u_cons(jj, units):
                    j = gi * 4 + jj
                    ur = j % 2
                    ub = (row_b[ur][0], row_b[ur][1])
                    g.act(rows[:, ur, 1:16], cpool[:, cp, k, j, :], AF.Copy, (cpool_b[cp][k],), ub)
                    for (ujj, cs, n, bk, _si) in units:
                        if ujj != jj:
                            continue
                        g.act(rows[:, ur, 16 + cs:16 + cs + n], ps[:, bk, 0:n], AF.Copy, (bank_b[bk],), ub)
                        g.tt(rows[:, 2, 16 + cs:16 + cs + n], ps[:, bk, 0:n], rows[:, ur, 15 + cs:15 + cs + n],
                             ALU.add, (bank_b[bk],) + ub, sa)
                    if w >= 4:
                        g.tt(rows[:, 2, 2:16], rows[:, ur, 2:16], rows[:, ur, 1:15], ALU.add, ub, sa)
                        g.tt(rows[:, 3, 4:EE], rows[:, 2, 4:EE], rows[:, 2, 2:EE - 2], ALU.add, sa, sbb)
                        S, Sb = 3, sbb
                    else:
                        S, Sb = 2, sa
                    if w >= 8:
                        g.tt(rows[:, 2, 8:EE], rows[:, 3, 8:EE], rows[:, 3, 4:EE - 4], ALU.add, sbb, sa)
                        S, Sb = 2, sa
                    if w >= 16:
                        g.tt(rows[:, 3, 16:EE], rows[:, 2, 16:EE], rows[:, 2, 8:EE - 8], ALU.add, sa, sbb)
                        S, Sb = 3, sbb
                    g.stt(dbuf[:, dp, jj, 0:T], rows[:, S, 16:EE], 1.0 / w, rows[:, ur, 16:EE],
                          ALU.mult, ALU.subtract, Sb + ub, (db_b[dp],))
                    if tile.is_prompt and tile.first:
                        nf = w - 1
                        g.tt(fix[:, 0:nf], rows[:, S, 16:16 + nf], cst[:, C_RC:C_RC + nf], ALU.mult,
                             Sb + (cst_b,), (fix_b,))
                        g.tt(dbuf[:, dp, jj, 0:nf], fix[:, 0:nf], rows[:, ur, 16:16 + nf], ALU.subtract,
                             (fix_b,) + ub, (db_b[dp],))
                    g.act(cpool[:, cp, k, j, :], rows[:, ur, T + 1:T + 16], AF.Copy, ub, (cpool_b[cp][k],))

                return u_pe, u_cons

            def zm_prep(gi, zrows=None):
                sz_ = next_block("bz")
                sg = next_block("bg")
                dp = gi % 2

                def zrow(j):
                    if zrows is not None:
                        return zrows[j % 4]
                    return 4 + (j % 2)

                def zpart(jj):
                    j = gi * 4 + jj
                    zr = zrow(j)
                    zb = (row_b[zr][0], row_b[zr][1])
                    for si, (cs, n) in enumerate(tile.segs):
                        bz = next_bank()
                        for kc in range(NC_D):
                            g.mm(ps[:, bz, 0:n], wr[:, sz_, (kc * 4 + jj) * 128:(kc * 4 + jj + 1) * 128],
                                 xb[:, kc, cs:cs + n], kc == 0, kc == NC_D - 1,
                                 (wr_b[sz_], xb_b[kc][si]), (bank_b[bz],), flag=(kc == NC_D - 1))
                        g.act(rows[:, zr, cs:cs + n], ps[:, bz, 0:n], AF.Silu, (bank_b[bz],), zb)

                def mpart(jj):
                    j = gi * 4 + jj
                    zr = zrow(j)
                    zb = (row_b[zr][0], row_b[zr][1])
                    for si, (cs, n) in enumerate(tile.segs):
                        bm = next_bank()
                        for kc in range(4):
                            g.mm(ps[:, bm, 0:n], wr[:, sg, (kc * 4 + jj) * 128:(kc * 4 + jj + 1) * 128],
                                 dbuf[:, dp, kc, cs:cs + n], kc == 0, kc == 3,
                                 (wr_b[sg], db_b[dp]), (bank_b[bm],), flag=(kc == 3))
                        g.stt(yv(tile, j, cs, n), ps[:, bm, 0:n], cc(C_SC + k * 16 + j), rows[:, zr, cs:cs + n],
                              ALU.mult, ALU.mult, (bank_b[bm], cst_b) + zb, (yB(tile, j),))

                def step(jj):
                    zpart(jj)
                    mpart(jj)
                step.zpart = zpart
                step.mpart = mpart
                return step

            mark("L%d u0" % l)
            u_pe, u_cons = u_prep(0)
            if staggered:
                ua = u_pe(None, False, pairs=[(0, 0)])
                run_pending()
                ua = ua + u_pe(None, False, pairs=[(1, 0), (2, 0)])
                ub_ = u_pe(None, True, pairs=[(0, 1), (1, 1), (2, 1)])
                for jj in range(3):
                    u_cons(jj, ua + ub_)
                units = u_pe([3], False)
                u_cons(3, units)
            else:
                units = u_pe([0, 1], True)
                u_cons(0, units)
                u_cons(1, units)
                for jj in (2, 3):
                    units = u_pe([jj], False)
                    u_cons(jj, units)
            yield
            for gi in range(3):
                mark("L%d u%d+zm%d" % (l, gi + 1, gi))
                u_pe, u_cons = u_prep(gi + 1)
                zstep = zm_prep(gi)
                if gi == 0 and staggered:
                    seq = [("u", 0), ("u", 1), ("z", 0), ("u", 2), ("z", 1), ("u", 3), ("z", 2), ("z", 3)]
                else:
                    seq = [(kind, jj) for jj in range(4) for kind in ("u", "z")]
                for kind, jj in seq:
                    if kind == "u":
                        units = u_pe([jj], False)
                        u_cons(jj, units)
                    else:
                        zstep(jj)
                yield
            mark("L%d zm3" % l)
            spare = 6 + (1 - tile.hb)
            zstep = zm_prep(3, zrows=[4, 5, spare, 4])
            zstep.zpart(0)
            zstep.zpart(1)
            zstep.zpart(2)
            zstep.mpart(0)
            zstep.zpart(3)
            zstep.mpart(1)
            zstep.mpart(2)
            zstep.mpart(3)
            yield

        def out_proj(tile):
            xb, xb_b = tile.cx.xb, tile.cx.xb_b
            for mb in range(4):
                s = next_block("out")
                for m2 in range(2):
                    m = mb * 2 + m2
                    for si, (cs, n) in enumerate(tile.segs):
                        bk = next_bank()
                        for kc in range(NC_E):
                            g.mm(ps[:, bk, 0:n], wr[:, s, (kc * 2 + m2) * 128:(kc * 2 + m2 + 1) * 128],
                                 yv(tile, kc, cs, n), kc == 0, kc == NC_E - 1,
                                 (wr_b[s], yB(tile, kc)), (bank_b[bk],), flag=(kc == NC_E - 1))
                        g.tt(hs(tile, m, si, n), ps[:, bk, 0:n], hs(tile, m, si, n), ALU.add,
                             (bank_b[bk], hB(tile, m)), (hB(tile, m),))
                        g.act(xb[:, m, cs:cs + n], hs(tile, m, si, n), AF.Copy, (hB(tile, m),), (xb_b[m][si],))
                yield

        def gate_unit(tile, l, m, si, spg, spe):
            xb, xb_b = tile.cx.xb, tile.cx.xb_b
            rows, row_b = tile.cx.rows, tile.cx.row_b
            pb, pb_b, s_p = tile.cx.pb, tile.cx.pb_b, tile.cx.s_p
            HALF = tile.cx.HALF
            par = l % 2
            cs, n = tile.segs[si]
            u = tile.cx.unit
            tile.cx.unit = u + 1
            hp = u % 2
            ho = hp * HALF
            bg = next_bank()
            sl = spg[m // 4]
            m4 = m % 4
            for kc in range(NC_D):
                g.mm(ps[:, bg, 0:n], wr[:, sl, (kc * 4 + m4) * 128:(kc * 4 + m4 + 1) * 128],
                     xb[:, kc, cs:cs + n], kc == 0, kc == NC_D - 1,
                     (wr_b[sl], xb_b[kc][si]), (bank_b[bg],), flag=(kc == NC_D - 1))
            be = next_bank()
            for kc in range(2):
                g.mm(ps[:, be, 0:n], wr[:, spe, (kc * 8 + m) * 128:(kc * 8 + m + 1) * 128],
                     pb[:, par, kc, cs:cs + n], kc == 0, kc == 1,
                     (wr_b[spe], pb_b[par]), (bank_b[be],), flag=(kc == 1))
            gt = rows[:, 2, ho:ho + n]
            tv = rows[:, 3, ho:ho + n]
            g.act(gt, ps[:, bg, 0:n], AF.Sigmoid, (bank_b[bg],), (row_b[2][hp],))
            g.tt(tv, ps[:, be, 0:n], gt, ALU.mult, (bank_b[be], row_b[2][hp]), (row_b[3][hp],))
            g.tt(hs(tile, m, si, n), hs(tile, m, si, n), tv, ALU.add,
                 (hB(tile, m), row_b[3][hp]), (hB(tile, m),))

        def gate_staggered(tile, l):
            spg = [next_block("pg"), next_block("pg")]
            spe = next_block("pe")
            ssc = stats_begin(tile, 0)
            units = [(si, m) for si in range(2) for m in range(NC_D)]
            naq = []
            for idx, (si, m) in enumerate(units):
                gate_unit(tile, l, m, si, spg, spe)
                if idx >= 1:
                    a_si, a_m = units[idx - 1]
                    stats_act_seg(ssc, a_m, a_si)
                if idx >= 2:
                    b_si, b_m = units[idx - 2]
                    stats_pe_seg(ssc, b_m, b_si)
                    if (b_si, b_m) == (0, NC_D - 1):
                        stats_end_seg(ssc, 0)
                        naq = norm_apply_jobs(tile, l + 1, 0)
                if naq:
                    naq.pop(0)()
            while naq:
                naq.pop(0)()
            stats_act_seg(ssc, NC_D - 1, 1)
            stats_pe_seg(ssc, NC_D - 2, 1)

            def tail_a():
                stats_pe_seg(ssc, NC_D - 1, 1)
                stats_end_seg(ssc, 1)

            def tail_b():
                norm_apply_seg(tile, l + 1, 1)
            state["pending"] += [tail_a, tail_b]

        def gate(tile, l, nxt):
            xb, xb_b = tile.cx.xb, tile.cx.xb_b
            rows, row_b = tile.cx.rows, tile.cx.row_b
            pb, pb_b, s_p = tile.cx.pb, tile.cx.pb_b, tile.cx.s_p
            HALF = tile.cx.HALF
            par = l % 2
            spg = [next_block("pg"), next_block("pg")]
            spe = next_block("pe")
            ssc = stats_begin(tile, 0)
            ssn = None
            for m in range(NC_D):
                if nxt is not None and m == 4:
                    ssn = stats_begin(nxt, 1)
                for si, (cs, n) in enumerate(tile.segs):
                    u = tile.cx.unit
                    tile.cx.unit = u + 1
                    hp = u % 2
                    ho = hp * HALF
                    bg = next_bank()
                    sl = spg[m // 4]
                    m4 = m % 4
                    for kc in range(NC_D):
                        g.mm(ps[:, bg, 0:n], wr[:, sl, (kc * 4 + m4) * 128:(kc * 4 + m4 + 1) * 128],
                             xb[:, kc, cs:cs + n], kc == 0, kc == NC_D - 1,
                             (wr_b[sl], xb_b[kc][si]), (bank_b[bg],), flag=(kc == NC_D - 1))
                    be = next_bank()
                    for kc in range(2):
                        g.mm(ps[:, be, 0:n], wr[:, spe, (kc * 8 + m) * 128:(kc * 8 + m + 1) * 128],
                             pb[:, par, kc, cs:cs + n], kc == 0, kc == 1,
                             (wr_b[spe], pb_b[par]), (bank_b[be],), flag=(kc == 1))
                    gt = rows[:, 2, ho:ho + n]
                    tv = rows[:, 3, ho:ho + n]
                    g.act(gt, ps[:, bg, 0:n], AF.Sigmoid, (bank_b[bg],), (row_b[2][hp],))
                    g.tt(tv, ps[:, be, 0:n], gt, ALU.mult, (bank_b[be], row_b[2][hp]), (row_b[3][hp],))
                    g.tt(hs(tile, m, si, n), hs(tile, m, si, n), tv, ALU.add,
                         (hB(tile, m), row_b[3][hp]), (hB(tile, m),))
                if m >= 1:
                    stats_act(ssc, m - 1)
                if m >= 2:
                    stats_pe(ssc, m - 2)
                if ssn is not None:
                    if m >= 5:
                        stats_pe(ssn, 2 * (m - 5))
                        stats_pe(ssn, 2 * (m - 5) + 1)
                    stats_act(ssn, 2 * (m - 4))
                    stats_act(ssn, 2 * (m - 4) + 1)
            stats_act(ssc, NC_D - 1)
            stats_pe(ssc, NC_D - 2)
            if ssn is not None:
                stats_pe(ssn, NC_D - 2)
                stats_pe(ssn, NC_D - 1)
            if tile.T == TP:
                keep_warm(WARM_A)
            stats_pe(ssc, NC_D - 1)
            if tile.T == TP:
                keep_warm(WARM_B)
            if ssn is not None:
                stats_end(ssn)
            stats_end(ssc)

        def final_job(tile, c):
            def job():
                for si, (cs, n) in enumerate(tile.segs):
                    g.stt(hs(tile, c, si, n), hs(tile, c, si, n), cc(C_FG + c), rr(tile, cs, n),
                          ALU.mult, ALU.mult, (hB(tile, c), cst_b) + rrB(tile), (hB(tile, c),))
                g.dma("sp", tile.y_dst[:, c], hfull(tile, c), s_yo[c], reads=(hB(tile, c),))
            return job

        def state_out_job(tile):
            def job():
                co, po = tile.st_out
                cp = tile.cpar
                g.dma("sp", co, cconv[:, cp, :, :, :], s_sto[0], reads=(cconv_b[cp][0], cconv_b[cp][1]))
                g.dma("sp", po, cpool[:, cp, :, :, :], s_sto[1], reads=(cpool_b[cp][0], cpool_b[cp][1]))
            return job

        ptiles = []
        for ti in range(NTILE_P):
            tl = Tile(TP, xp[ti], [pp[ti, l] for l in range(DEPTH)], yp[ti], ti % 2 == 0, ti % 2 == 1, True,
                      (cop[ti // 2], pop[ti // 2]))
            tl.hb = ti % 2
            tl.cpar = (ti // 2) % 2
            tl.cx = cxp
            ptiles.append(tl)
        smp = Tile(TS, xs, [psm[l] for l in range(DEPTH)], ys, True, True, False, (cos, pos))
        smp.hb = 0
        smp.cpar = 0
        smp.cx = cxs

        def run(gm, gc):
            if gc is None:
                for _ in gm:
                    pass
                return
            state["logging"] = True
            for _ in gm:
                state["replay"] = True
                while state["log"]:
                    next(gc)
                state["replay"] = False
            state["logging"] = False
            for _ in gc:
                raise AssertionError("co-tile generator out of sync")

        def mixer(tile, l, stg):
            return mixer_a(tile, l, stg) if l % 2 == 0 else mixer_b(tile, l, stg)

        def prep_tile(tile):
            for c in range(NC_D):
                load_x(tile, c)
            init_carries(tile)
            load_p(tile, 0)
            ss0 = stats_begin(tile, 0)
            for c in range(NC_D):
                stats_act(ss0, c)
                stats_pe(ss0, c)
            stats_end(ss0)
            norm_apply(tile, 0)

        prep_tile(ptiles[0])
        for i, tile in enumerate(ptiles):
            nxt = ptiles[i + 1] if i + 1 < len(ptiles) else None
            co = smp if i == len(ptiles) - 1 else None
            mark("tile start")
            for l in range(DEPTH):
                stg = (l > 0) and stag(tile)
                if l > 0 and not stg:
                    norm_apply(tile, l)
                if co is not None and l > 0:
                    norm_apply(co, l)
                mark("L%d mixer" % l)
                run(mixer(tile, l, stg), mixer(co, l, False) if co is not None else None)
                assert not state["pending"]
                flush_side()
                if i == len(ptiles) - 2 and l == 0:
                    prep_tile(smp)
                if l + 1 < DEPTH:
                    load_p(tile, l + 1)
                    if co is not None:
                        load_p(co, l + 1)
                elif nxt is not None:
                    load_p(nxt, 0)
                mark("L%d out" % l)
                run(out_proj(tile), out_proj(co) if co is not None else None)
                last_l = (l == DEPTH - 1)
                if last_l and nxt is not None:
                    for c in range(NC_D):
                        load_x(nxt, c)
                    if nxt.first:
                        init_carries(nxt)
                mark("L%d gate" % l)
                state["logging"] = co is not None
                if (not last_l) and stag(tile):
                    gate_staggered(tile, l)
                else:
                    gate(tile, l, nxt if last_l else None)
                state["logging"] = False
                if co is not None:
                    state["replay"] = True
                    gate(co, l, None)
                    state["replay"] = False
                    assert not state["log"]
            mark("final")
            jobs = [final_job(tile, c) for c in range(NC_D)]
            if tile.last:
                jobs.append(state_out_job(tile))
            if nxt is not None:
                norm_apply_seg(nxt, 0, 0)
                jobs.pop(0)()
                jobs.pop(0)()
                norm_apply_seg(nxt, 0, 1)
                state["side"] = jobs
            else:
                for jb in jobs:
                    jb()
            if co is not None:
                for c in range(NC_D):
                    final_job(co, c)()
                state_out_job(co)()
        g.finish("sp")

        @block.tensor
        def _(t):
            g.replay("pe", t)

        @block.scalar
        def _(a):
            g.replay("act", a)

        @block.vector
        def _(v):
            g.replay("dve", v)

        @block.gpsimd
        def _(p):
            g.replay("pool", p)

        @block.sync
        def _(s):
            g.replay("sp", s)

    return nc


_CACHE = {}
MARKS = []


def _kc_split(W):
    K, M = W.shape
    return W.reshape(K // 128, 128, M).transpose(1, 0, 2)


def build_wstream(w_in_a, w_out_a, w_in_b, w_grp_b, w_out_b, w_pe, w_pg):
    out = np.empty((128, WTOT), dtype=np.float32)
    for bi, (kind, l, idx, n) in enumerate(SCHED):
        k = l // 2
        if kind == "ain":
            W = _kc_split(w_in_a[k]).reshape(128, 8, 4, 16, 128)
            blk = W[:, :, :, idx, :]
        elif kind == "bu":
            W = _kc_split(w_in_b[k]).reshape(128, 8, 2, 4, 4, 128)
            blk = W[:, :, 0, idx]
        elif kind == "bz":
            W = _kc_split(w_in_b[k]).reshape(128, 8, 2, 4, 4, 128)
            blk = W[:, :, 1, idx]
        elif kind == "bg":
            blk = _kc_split(w_grp_b[k][idx])
        elif kind == "out":
            Wo = w_out_a[k] if l % 2 == 0 else w_out_b[k]
            blk = _kc_split(Wo).reshape(128, 16, 4, 2, 128)[:, :, idx]
        elif kind == "pg":
            blk = _kc_split(w_pg[l]).reshape(128, 8, 2, 4, 128)[:, :, idx]
        elif kind == "pe":
            blk = _kc_split(w_pe[l])
        o = int(WOFF[bi])
        out[:, o:o + n] = np.ascontiguousarray(blk).reshape(128, n)
    return out


def kernel(x_prompt, x_sample, state_conv, state_pool, p_prompt, p_sample, norm_g,
           w_in_a, conv_w_a, w_out_a, w_in_b, w_grp_b, scale_b, w_out_b, w_pe, w_pg, final_g):
    f = lambda a: np.asarray(a, dtype=np.float32)
    x_prompt, x_sample, state_conv, state_pool = f(x_prompt), f(x_sample), f(state_conv), f(state_pool)
    p_prompt, p_sample = f(p_prompt), f(p_sample)
    norm_g, conv_w_a, scale_b, final_g = f(norm_g), f(conv_w_a), f(scale_b), f(final_g)

    if "nc" not in _CACHE:
        _CACHE["nc"] = build_program()
    nc = _CACHE["nc"]

    wst = build_wstream(f(w_in_a), f(w_out_a), f(w_in_b), f(w_grp_b), f(w_out_b), f(w_pe), f(w_pg))
    cst = np.zeros((128, NCONST), np.float32)
    cst[:, C_G:C_G + 32] = norm_g.reshape(4, 8, 128).transpose(2, 0, 1).reshape(128, 32)
    cst[:, C_FG:C_FG + 8] = final_g.reshape(8, 128).T
    cst[:, C_CW:C_CW + 96] = conv_w_a.reshape(2, 3, 16, 128).transpose(3, 0, 1, 2).reshape(128, 96)
    cst[:, C_SC:C_SC + 32] = scale_b.reshape(2, 16, 128).transpose(2, 0, 1).reshape(128, 32)
    cst[:, C_RC:C_RC + 16] = (1.0 / np.arange(1, 17, dtype=np.float64)).astype(np.float32)[None, :]

    in_maps = []
    for i in range(NCORE):
        xb_ = x_prompt[4 * i:4 * i + 4]
        xpi = xb_.reshape(4, 2, TP, NC_D, 128).transpose(0, 1, 4, 3, 2).reshape(NTILE_P, 128, NC_D, TP)
        ppi = p_prompt[:, 4 * i:4 * i + 4].reshape(DEPTH, 4, 2, TP, 2, 128).transpose(1, 2, 0, 5, 4, 3)
        ppi = ppi.reshape(NTILE_P, DEPTH, 128, 2, TP)
        xsi = x_sample[i].reshape(TS, NC_D, 128).transpose(2, 1, 0)
        psi = p_sample[:, i].reshape(DEPTH, TS, 2, 128).transpose(0, 3, 2, 1)
        sci = state_conv[:, i].reshape(2, 2, NC_E, 128).transpose(3, 0, 2, 1)
        spi = state_pool[:, i].reshape(2, 15, NC_E, 128).transpose(3, 0, 2, 1)
        in_maps.append({
            "xp": np.ascontiguousarray(xpi).reshape(NTILE_P, 128, NC_D, 2, 512), "xs": np.ascontiguousarray(xsi),
            "pp": np.ascontiguousarray(ppi), "psm": np.ascontiguousarray(psi),
            "sci": np.ascontiguousarray(sci), "spi": np.ascontiguousarray(spi),
            "cst": cst, "wst": wst,
        })

    res = run_bass_kernel_spmd(nc, in_maps, core_ids=list(range(NCORE)))
    R = res.results

    y_prompt = np.empty((32, SEQ, D), np.float32)
    y_sample = np.empty((8, TS, D), np.float32)
    conv_p = np.empty((2, 32, 2, E), np.float32)
    pool_p = np.empty((2, 32, 15, E), np.float32)
    conv_s = np.empty((2, 8, 2, E), np.float32)
    pool_s = np.empty((2, 8, 15, E), np.float32)
    for i in range(NCORE):
        r = R[i]
        ypi = np.asarray(r["yp"]).reshape(4, 2, 128, NC_D, TP).transpose(0, 1, 4, 3, 2).reshape(4, SEQ, D)
        y_prompt[4 * i:4 * i + 4] = ypi
        y_sample[i] = np.asarray(r["ys"]).transpose(2, 1, 0).reshape(TS, D)
        cop = np.asarray(r["cop"])
        conv_p[:, 4 * i:4 * i + 4] = cop.transpose(2, 0, 4, 3, 1).reshape(2, 4, 2, E)
        pop = np.asarray(r["pop"])
        pool_p[:, 4 * i:4 * i + 4] = pop.transpose(2, 0, 4, 3, 1).reshape(2, 4, 15, E)
        conv_s[:, i] = np.asarray(r["cos"]).transpose(1, 3, 2, 0).reshape(2, 2, E)
        pool_s[:, i] = np.asarray(r["pos"]).transpose(1, 3, 2, 0).reshape(2, 15, E)
    return (y_prompt, y_sample, conv_p, pool_p, conv_s, pool_s)
```
